# Optimizing a Trainium2 kernel written in Bass

```python
import math
import jax, jax.numpy as jnp
from jax import lax
import numpy as np

D_MODEL = 2048
BATCH = 4
SEQ = 2048
DEPTH = 1
DEC_BATCH = 128
DEC_SEQ = 4
PAST_LEN = 16384
PAGE_SIZE = 128

SSD_WIDTH = D_MODEL
SSD_HEAD_DIM = 64
SSD_HEADS = SSD_WIDTH // SSD_HEAD_DIM
SSD_GROUPS = 4
SSD_STATE = 128
SSD_CONV = 4
SSD_CHUNK = 128
SSD_CONV_DIM = SSD_WIDTH + 2 * SSD_GROUPS * SSD_STATE
CF_WIDTH = D_MODEL
CF_CONV = 31
D_MIX = SSD_WIDTH + CF_WIDTH
IN_PROJ_DIM = SSD_WIDTH + SSD_CONV_DIM + SSD_HEADS + 2 * CF_WIDTH
FFN_DIM = ((8 * D_MODEL // 3 + 127) // 128) * 128
FFN_CONV = 3
EPS = 1e-5

kernel_name = "hymba_ssd_conformer_convffn_step"


def rmsnorm(x, w):
    xf = x.astype(jnp.float32)
    r = lax.rsqrt(jnp.mean(xf * xf, axis=-1, keepdims=True) + EPS)
    return (xf * r).astype(x.dtype) * w


def layernorm(x, w, b):
    xf = x.astype(jnp.float32)
    mu = jnp.mean(xf, axis=-1, keepdims=True)
    var = jnp.mean(jnp.square(xf - mu), axis=-1, keepdims=True)
    return ((xf - mu) * lax.rsqrt(var + EPS)).astype(x.dtype) * w + b


def causal_dwconv(x, buf, w, b):
    k = w.shape[0]
    l = x.shape[1]
    xp = jnp.concatenate([buf.astype(x.dtype), x], axis=1)
    y = xp[:, 0:l] * w[0]
    for i in range(1, k):
        y = y + xp[:, i:i + l] * w[i]
    return y + b, xp[:, l:]


def ssd_scan(x, dt, A, B, C, h0):
    b, l = x.shape[0], x.shape[1]
    q = SSD_CHUNK if l % SSD_CHUNK == 0 else l
    c = l // q
    G, R, P, N = SSD_GROUPS, SSD_HEADS // SSD_GROUPS, SSD_HEAD_DIM, SSD_STATE
    xdt = (x * dt[..., None].astype(x.dtype)).reshape(b, c, q, G, R, P)
    dA = (dt.astype(jnp.float32) * A.astype(jnp.float32)).reshape(b, c, q, G, R)
    cs = jnp.cumsum(dA, axis=2)
    Bc = B.reshape(b, c, q, G, N)
    Cc = C.reshape(b, c, q, G, N)
    mask = jnp.tril(jnp.ones((q, q), dtype=bool))[:, :, None, None]
    seg = cs[:, :, :, None] - cs[:, :, None]
    Lmat = jnp.exp(jnp.where(mask, seg, -jnp.inf))
    CB = jnp.einsum('bcign,bcjgn->bcijg', Cc, Bc)
    M = CB[..., None] * Lmat
    y_diag = jnp.einsum('bcijgr,bcjgrp->bcigrp', M, xdt)
    decay = jnp.exp(cs[:, :, -1:] - cs)
    chunk_states = jnp.einsum('bcjgn,bcjgrp->bcgrpn', Bc,
                              xdt * decay[..., None]).astype(jnp.float32)
    chunk_decay = jnp.exp(cs[:, :, -1])

    def step(h, inp):
        dec, st = inp
        return h * dec[..., None, None] + st, h

    h_init = h0.reshape(b, G, R, P, N).astype(jnp.float32)
    h_final, h_prev = lax.scan(step, h_init,
                               (jnp.swapaxes(chunk_decay, 0, 1), jnp.swapaxes(chunk_states, 0, 1)))
    h_prev = jnp.swapaxes(h_prev, 0, 1)
    y_off = jnp.einsum('bcign,bcgrpn->bcigrp', Cc, h_prev) * jnp.exp(cs)[..., None]
    y = (y_diag + y_off).reshape(b, l, SSD_HEADS, P)
    return y.astype(x.dtype), h_final.reshape(b, SSD_HEADS, P, N).astype(h0.dtype)


def mixer(h, ssm0, ssd_buf0, cf_buf0, w_in, ssd_conv_w, ssd_conv_b, dt_bias, a_log, d_skip,
          ssd_norm_w, cf_conv_w, cf_conv_b, cf_ln_w, cf_ln_b, w_out):
    b, l = h.shape[0], h.shape[1]
    proj = h @ w_in
    s1 = SSD_WIDTH
    s2 = s1 + SSD_CONV_DIM
    s3 = s2 + SSD_HEADS
    s4 = s3 + CF_WIDTH
    z, xbc, dt, cf_a, cf_g = jnp.split(proj, [s1, s2, s3, s4], axis=-1)
    xbc, ssd_buf = causal_dwconv(xbc, ssd_buf0, ssd_conv_w, ssd_conv_b)
    xbc = jax.nn.silu(xbc)
    xs, Bm, Cm = jnp.split(xbc, [SSD_WIDTH, SSD_WIDTH + SSD_GROUPS * SSD_STATE], axis=-1)
    dt = jax.nn.softplus((dt + dt_bias).astype(jnp.float32))
    A = -jnp.exp(a_log.astype(jnp.float32))
    xh = xs.reshape(b, l, SSD_HEADS, SSD_HEAD_DIM)
    y, ssm = ssd_scan(xh, dt, A, Bm.reshape(b, l, SSD_GROUPS, SSD_STATE),
                      Cm.reshape(b, l, SSD_GROUPS, SSD_STATE), ssm0)
    y = y + d_skip[:, None] * xh
    y = rmsnorm(y.reshape(b, l, SSD_WIDTH) * jax.nn.silu(z), ssd_norm_w)
    u = cf_a * jax.nn.sigmoid(cf_g)
    u, cf_buf = causal_dwconv(u, cf_buf0, cf_conv_w, cf_conv_b)
    u = jax.nn.silu(layernorm(u, cf_ln_w, cf_ln_b))
    out = jnp.concatenate([y.astype(h.dtype), u], axis=-1) @ w_out
    return out, ssm, ssd_buf, cf_buf


def conv_ffn(h, buf0, w_up, conv_w, conv_b, w_down):
    u = h @ w_up
    u, buf = causal_dwconv(u, buf0, conv_w, conv_b)
    g, v = jnp.split(u, [FFN_DIM], axis=-1)
    return (jax.nn.silu(g) * v) @ w_down, buf


def setup_inputs(seed: int = 0) -> dict:
    key = jax.random.key(seed)
    ks = jax.random.split(key, 28)
    f32 = jnp.float32
    nrm = lambda k, shape, s: jax.random.normal(k, shape, f32) * s
    H = SSD_HEADS
    dt0 = jnp.exp(jax.random.uniform(ks[0], (DEPTH, H), f32, math.log(1e-3), math.log(1e-1)))
    return {
        "x_prompt": nrm(ks[1], (BATCH, SEQ, D_MODEL), 1.0),
        "x_sample": nrm(ks[2], (DEC_BATCH, DEC_SEQ, D_MODEL), 1.0),
        "state_ssm": nrm(ks[3], (DEPTH, DEC_BATCH, H, SSD_HEAD_DIM, SSD_STATE), 0.1),
        "state_ssd_conv": nrm(ks[4], (DEPTH, DEC_BATCH, SSD_CONV - 1, SSD_CONV_DIM), 1.0),
        "state_cf_conv": nrm(ks[5], (DEPTH, DEC_BATCH, CF_CONV - 1, CF_WIDTH), 0.5),
        "state_ffn_conv": nrm(ks[6], (DEPTH, DEC_BATCH, FFN_CONV - 1, 2 * FFN_DIM), 1.0),
        "norm_mix_w": 1.0 + nrm(ks[7], (DEPTH, D_MODEL), 0.02),
        "w_in": nrm(ks[8], (DEPTH, D_MODEL, IN_PROJ_DIM), D_MODEL ** -0.5),
        "ssd_conv_w": nrm(ks[9], (DEPTH, SSD_CONV, SSD_CONV_DIM), SSD_CONV ** -0.5),
        "ssd_conv_b": nrm(ks[10], (DEPTH, SSD_CONV_DIM), 0.02),
        "ssd_dt_bias": dt0 + jnp.log(-jnp.expm1(-dt0)),
        "ssd_a_log": jnp.log(jax.random.uniform(ks[11], (DEPTH, H), f32, 1.0, 16.0)),
        "ssd_d": 1.0 + nrm(ks[12], (DEPTH, H), 0.1),
        "ssd_norm_w": 1.0 + nrm(ks[13], (DEPTH, SSD_WIDTH), 0.02),
        "cf_conv_w": nrm(ks[14], (DEPTH, CF_CONV, CF_WIDTH), CF_CONV ** -0.5),
        "cf_conv_b": nrm(ks[15], (DEPTH, CF_WIDTH), 0.02),
        "cf_ln_w": 1.0 + nrm(ks[16], (DEPTH, CF_WIDTH), 0.02),
        "cf_ln_b": nrm(ks[17], (DEPTH, CF_WIDTH), 0.02),
        "w_out": nrm(ks[18], (DEPTH, D_MIX, D_MODEL), D_MIX ** -0.5),
        "norm_ffn_w": 1.0 + nrm(ks[19], (DEPTH, D_MODEL), 0.02),
        "w_up": nrm(ks[20], (DEPTH, D_MODEL, 2 * FFN_DIM), D_MODEL ** -0.5),
        "ffn_conv_w": nrm(ks[21], (DEPTH, FFN_CONV, 2 * FFN_DIM), FFN_CONV ** -0.5),
        "ffn_conv_b": nrm(ks[22], (DEPTH, 2 * FFN_DIM), 0.02),
        "w_down": nrm(ks[23], (DEPTH, FFN_DIM, D_MODEL), FFN_DIM ** -0.5),
        "norm_final_w": 1.0 + nrm(ks[24], (D_MODEL,), 0.02),
    }


def reference(x_prompt, x_sample, state_ssm, state_ssd_conv, state_cf_conv, state_ffn_conv,
              norm_mix_w, w_in, ssd_conv_w, ssd_conv_b, ssd_dt_bias, ssd_a_log, ssd_d, ssd_norm_w,
              cf_conv_w, cf_conv_b, cf_ln_w, cf_ln_b, w_out, norm_ffn_w, w_up, ffn_conv_w,
              ffn_conv_b, w_down, norm_final_w):
    bp = x_prompt.shape[0]
    dtp = x_prompt.dtype
    xp, xs = x_prompt, x_sample
    p_ssm_l, p_ssdc_l, p_cfc_l, p_ffc_l = [], [], [], []
    s_ssm_l, s_ssdc_l, s_cfc_l, s_ffc_l = [], [], [], []
    for i in range(DEPTH):
        mix_w = (w_in[i], ssd_conv_w[i], ssd_conv_b[i], ssd_dt_bias[i], ssd_a_log[i], ssd_d[i],
                 ssd_norm_w[i], cf_conv_w[i], cf_conv_b[i], cf_ln_w[i], cf_ln_b[i], w_out[i])
        ffn_w = (w_up[i], ffn_conv_w[i], ffn_conv_b[i], w_down[i])
        p_ssm0 = jnp.zeros((bp, SSD_HEADS, SSD_HEAD_DIM, SSD_STATE), dtp)
        p_ssdc0 = jnp.zeros((bp, SSD_CONV - 1, SSD_CONV_DIM), dtp)
        p_cfc0 = jnp.zeros((bp, CF_CONV - 1, CF_WIDTH), dtp)
        p_ffc0 = jnp.zeros((bp, FFN_CONV - 1, 2 * FFN_DIM), dtp)
        m, p_ssm, p_ssdc, p_cfc = mixer(rmsnorm(xp, norm_mix_w[i]), p_ssm0, p_ssdc0, p_cfc0, *mix_w)
        xp = xp + m
        f, p_ffc = conv_ffn(rmsnorm(xp, norm_ffn_w[i]), p_ffc0, *ffn_w)
        xp = xp + f
        m, s_ssm, s_ssdc, s_cfc = mixer(rmsnorm(xs, norm_mix_w[i]), state_ssm[i], state_ssd_conv[i],
                                        state_cf_conv[i], *mix_w)
        xs = xs + m
        f, s_ffc = conv_ffn(rmsnorm(xs, norm_ffn_w[i]), state_ffn_conv[i], *ffn_w)
        xs = xs + f
        p_ssm_l.append(p_ssm); p_ssdc_l.append(p_ssdc); p_cfc_l.append(p_cfc); p_ffc_l.append(p_ffc)
        s_ssm_l.append(s_ssm); s_ssdc_l.append(s_ssdc); s_cfc_l.append(s_cfc); s_ffc_l.append(s_ffc)
    y_prompt = rmsnorm(xp, norm_final_w)
    y_sample = rmsnorm(xs, norm_final_w)
    p_ssm = jnp.stack(p_ssm_l)
    p_ssd_conv = jnp.stack(p_ssdc_l)
    p_cf_conv = jnp.stack(p_cfc_l)
    p_ffn_conv = jnp.stack(p_ffc_l)
    s_ssm = jnp.stack(s_ssm_l)
    s_ssd_conv = jnp.stack(s_ssdc_l)
    s_cf_conv = jnp.stack(s_cfc_l)
    s_ffn_conv = jnp.stack(s_ffc_l)
    return (y_prompt, y_sample, p_ssm, p_ssd_conv, p_cf_conv, p_ffn_conv,
            s_ssm, s_ssd_conv, s_cf_conv, s_ffn_conv)
```

```python
import numpy as np
import concourse.bass as bass
import concourse.mybir as mybir
from concourse.bass_utils import run_bass_kernel_spmd

F32 = mybir.dt.float32
BF16 = mybir.dt.bfloat16
AF = mybir.ActivationFunctionType
ALU = mybir.AluOpType

D = 2048
NKD = 16
H = 32
HP = 64
NG = 4
XBC = 3072
NSX = 24
INP = 9248
FFN = 5504
NKF = 43
EPS = 1e-5
SEQ = 2048
NSEQ_S = 16
TS = 64
C_Z, C_XBC, C_DT, C_CFA, C_CFG = 0, 2048, 5120, 5152, 7200


class Res:
    __slots__ = ("name", "w", "r", "excl", "strict")

    def __init__(self, name, excl=False, strict=False):
        self.name = name
        self.w = None
        self.r = {}
        self.excl = excl
        self.strict = strict


class FW:
    def __init__(self, nc):
        self.nc = nc
        self.E = {}
        for name, h in [("pe", nc.tensor), ("act", nc.scalar), ("dve", nc.vector),
                        ("pool", nc.gpsimd), ("sp", nc.sync)]:
            sem = nc.semaphore("c_" + name).__enter__()
            self.E[name] = dict(h=h, sem=sem, cnt=0, waited={}, pr=[], pw=[])
        self.streams = {}
        self.nwait = 0

    def stream(self, name):
        if name not in self.streams:
            self.streams[name] = dict(sem=self.nc.semaphore("d_" + name).__enter__(), n=0)
        return self.streams[name]

    def _deps(self, eng, reads, writes):
        deps = {}

        def add(ev, raw, strict=False):
            if ev is None:
                return
            sem, val, src = ev
            if src == eng:
                if eng == "pe" or eng == "sp":
                    return
                if not raw and not strict:
                    return
            k = id(sem)
            if k not in deps or deps[k][1] < val:
                deps[k] = (sem, val)

        for r in reads:
            add(r.w, True)
            if r.excl:
                for ev in r.r.values():
                    add(ev, False)
        for w in writes:
            add(w.w, w.excl, w.strict)
            for ev in w.r.values():
                add(ev, False, w.strict)
        return deps

    def _wait(self, eng, deps):
        E = self.E[eng]
        for k, (sem, val) in deps.items():
            if E["waited"].get(k, 0) < val:
                E["h"].wait_ge(sem, val)
                E["waited"][k] = val
                self.nwait += 1

    def _commit(self, ev, reads, writes):
        k = id(ev[0])
        for r in reads:
            if r.excl:
                r.w = ev
                r.r = {}
            else:
                r.r[k] = ev
        for w in writes:
            w.w = ev
            w.r = {}

    def op(self, eng, fn, r=(), w=(), sig=True):
        E = self.E[eng]
        self._wait(eng, self._deps(eng, r, w))
        inst = fn(E["h"])
        if sig:
            E["cnt"] += 1
            inst.then_inc(E["sem"], 1)
            ev = (E["sem"], E["cnt"], eng)
            self._commit(ev, list(r) + E["pr"], list(w) + E["pw"])
            E["pr"], E["pw"] = [], []
            return ev
        E["pr"] += list(r)
        E["pw"] += list(w)
        return None

    def dma(self, q, out, in_, r=(), w=(), stream=None):
        E = self.E[q]
        if stream is None:
            rs = list(w) + list(r)
            assert len(rs) == 1, "dma with several resources needs an explicit private stream"
            stream = rs[0].name
        st = self.stream(stream)
        self._wait(q, self._deps("dma", r, w))
        E["h"].dma_start(out=out, in_=in_).then_inc(st["sem"], 16)
        st["n"] += 16
        ev = (st["sem"], st["n"], "dma:" + stream)
        self._commit(ev, r, w)
        return ev

    def barrier(self, engs=("pe", "act", "dve", "sp")):
        for e in engs:
            self.wait_all(e)

    def wait_all(self, eng):
        deps = {}
        for n, E in self.E.items():
            if E["cnt"] > 0 and n != eng:
                deps[id(E["sem"])] = (E["sem"], E["cnt"])
        for st in self.streams.values():
            if st["n"] > 0:
                deps[id(st["sem"])] = (st["sem"], st["n"])
        self._wait(eng, deps)


def build_program(passes, with_sample=True, dbg=None):
    nc = bass.Bass("TRN2", target_bir_lowering=False)
    fw = FW(nc)
    dbg = dbg or []

    def dram_in(name, shape):
        return nc.dram_tensor(name, list(shape), F32, kind="ExternalInput").ap()

    def dram_out(name, shape):
        return nc.dram_tensor(name, list(shape), F32, kind="ExternalOutput").ap()

    def _has_s(p):
        return bool(p.get("sample")) and with_sample
    NSLOT = max(len(p["chunks"]) + (1 if _has_s(p) else 0) for p in passes)
    TT = max(len(p["chunks"]) * 128 + (TS if _has_s(p) else 0) for p in passes)
    PRE_N = 30 + max(len(p["chunks"]) * 128 + (NSEQ_S * 34 if _has_s(p) else 0) for p in passes)
    NCH_IN = max(max(p["chunks"]) for p in passes if p.get("mode") != "prefix") + 1
    NCH_PRE = max([max(p["chunks"]) + 1 for p in passes if p.get("mode") == "prefix"] + [1])
    NCH_OUT = max(max([y for y in p.get("yrows", p["chunks"]) if y is not None] + [0]) for p in passes if p.get("mode") != "prefix") + 1
    xp = dram_in("xp", [NCH_IN * 128, D])
    xpre = dram_in("xpre", [NCH_PRE * 128, D])
    c_flag = dram_in("c_flag", [128])
    xsd = dram_in("xs", [TS, D])
    w_in = dram_in("w_in", [D, INP])
    w_out = dram_in("w_out", [2 * D, D])
    w_up = dram_in("w_up", [D, 2 * FFN])
    w_down = dram_in("w_down", [FFN, D])
    prm_norm = dram_in("prm_norm", [3, D])
    prm_ssd = dram_in("prm_ssd", [5, XBC])
    prm_cf = dram_in("prm_cf", [34, D])
    prm_ffn = dram_in("prm_ffn", [4, 2 * FFN])
    prm_head = dram_in("prm_head", [3, H])
    norm_fin = dram_in("norm_fin", [D])
    c_ident = dram_in("c_ident", [128, 128])
    c_U = dram_in("c_U", [128, 128])
    c_smp = dram_in("c_smp", [128, 512])
    c_selbc = dram_in("c_selbc", [NSEQ_S * TS])
    st_ssm = dram_in("st_ssm", [NSEQ_S, D, 128])
    st_ssdc = dram_in("st_ssdc", [NSEQ_S * 3, XBC])
    st_cfc = dram_in("st_cfc", [NSEQ_S * 30, D])
    st_ffc = dram_in("st_ffc", [NSEQ_S * 2, 2 * FFN])

    y_p = dram_out("y_p", [NCH_OUT * 128, D])
    y_s = dram_out("y_s", [TS, D])
    o_pssm = dram_out("o_pssm", [D, 128])
    o_pssdc = dram_out("o_pssdc", [3, XBC])
    o_pcfc = dram_out("o_pcfc", [30, D])
    o_pffc = dram_out("o_pffc", [2, 2 * FFN])
    o_sssm = dram_out("o_sssm", [NSEQ_S, D, 128])
    o_sssdc = dram_out("o_sssdc", [NSEQ_S * 3, XBC])
    o_scfc = dram_out("o_scfc", [NSEQ_S * 30, D])
    o_sffc = dram_out("o_sffc", [NSEQ_S * 2, 2 * FFN])
    dbg_out = {}

    def sb(name, shape, dt=F32):
        return nc.sbuf_tensor(name, list(shape), dt).__enter__()

    pb = [nc.psum_tensor("pb%d" % i, [128, 512], F32).__enter__() for i in range(8)]
    pbr = [Res("pb%d" % i, excl=True) for i in range(8)]

    def pbf(i):
        return pb[i][:].bitcast(BF16)


    n_r1 = NSLOT * D * 2
    n_r2 = NKD * TT
    KH = 22
    n_r3 = NSX * TT
    arena = sb("arena", [128, n_r1 + n_r2 + n_r3], BF16)
    o1, o2, o3 = 0, n_r1, n_r1 + n_r2
    x1 = arena[:, o1:o1 + n_r1].bitcast(F32).rearrange("p (c f) -> p c f", c=NSLOT)
    hT = arena[:, o1:o1 + NKD * TT].rearrange("p (k t) -> p k t", k=NKD)
    zs = arena[:, o1 + NKD * TT:o1 + NKD * TT + NSLOT * D].rearrange("p (c f) -> p c f", c=NSLOT)
    assert NKD * TT + NSLOT * D <= n_r1
    mixcf = arena[:, o2:o2 + n_r2].rearrange("p (k t) -> p k t", k=NKD)
    h2T = mixcf
    gv = arena[:, o3:o3 + KH * TT].rearrange("p (k t) -> p k t", k=KH)
    xc = arena[:, o3:o3 + NSX * TT].rearrange("p (k t) -> p k t", k=NSX)

    r_x1 = [Res("x1_%d" % c) for c in range(NSLOT)]
    r_hT = [Res("hT_%d" % c) for c in range(NSLOT)]
    r_h2T = [Res("h2T_%d" % c) for c in range(NSLOT)]
    r_zs = [Res("zs_%d" % c) for c in range(NSLOT)]
    r_xc = [Res("xc_%d" % s) for s in range(NSX)]
    r_mixcf = [Res("mixcf_%d" % s) for s in range(NKD)]
    r_gv = [Res("gv_%d" % s) for s in range(KH)]

    ident_f = sb("ident_f", [128, 128]); r_ident = Res("ident")
    ident_b = sb("ident_b", [128, 128], BF16)
    U_f = sb("U_f", [128, 128]); r_U = Res("U")
    ones_f = sb("ones_f", [128, 128]); ones_b = sb("ones_b", [128, 128], BF16); r_ones = Res("ones")
    epsb = sb("epsb", [128, 1]); r_eps = Res("eps")
    pn_fm = sb("pn_fm", [128, NKD, 3]); r_pn = Res("pn")
    pssd_fm = sb("pssd_fm", [128, NSX, 5]); r_pssd = Res("pssd")
    pcf_fm = sb("pcf_fm", [128, NKD, 34]); r_pcf = Res("pcf")
    pffn_fm = sb("pffn_fm", [128, 2 * NKF, 4]); r_pffn = Res("pffn")
    head_bc = sb("head_bc", [128, 3, H]); r_head = Res("head")
    r_wfin = Res("wfin")
    wdt = sb("wdt", [128, NKD, 32], BF16); r_wdt = Res("wdt")
    stg = sb("stg", [128, 512]); r_stg = Res("stg")
    flg = sb("flg", [128, 1]); r_flg = Res("flg")
    fw.dma("sp", flg[:], c_flag.rearrange("(p o) -> p o", o=1), w=[r_flg])

    fw.dma("sp", ident_f[:], c_ident, w=[r_ident])
    fw.dma("sp", U_f[:], c_U, w=[r_U])
    fw.dma("sp", head_bc[:].rearrange("p a h -> p (a h)"), prm_head.rearrange("a h -> (a h)").partition_broadcast(128), w=[r_head])
    fw.dma("pool", wdt[:], w_in[:, C_DT:C_DT + 32].rearrange("(k p) c -> p k c", p=128), w=[r_wdt])
    fw.op("dve", lambda h: h.tensor_copy(out=ident_b[:], in_=ident_f[:]), r=[r_ident], w=[r_ident])
    fw.op("dve", lambda h: h.memset(ones_f[:], 1.0), w=[r_ones])
    fw.op("dve", lambda h: h.memset(ones_b[:], 1.0), w=[r_ones])
    fw.op("dve", lambda h: h.memset(epsb[:], EPS), w=[r_eps])
    fw.op("act", lambda h: h.activation(out=head_bc[:, 1, :], in_=head_bc[:, 1, :], func=AF.Exp), r=[r_head], w=[r_head])
    fw.op("dve", lambda h: h.tensor_scalar(out=head_bc[:, 1, :], in0=head_bc[:, 1, :], scalar1=-1.0, scalar2=None, op0=ALU.mult), r=[r_head], w=[r_head])

    def load_rows_fm(src2d, R, C, dst_fm, r_dst, bank):
        for c0 in range(0, C, 512):
            cw = min(512, C - c0)
            ns = cw // 128
            fw.dma("sp", stg[0:R, 0:cw], src2d[:, c0:c0 + cw], w=[r_stg])
            for j in range(ns):
                fw.op("pe", lambda h, j=j: h.transpose(out=pb[bank][:, j * R:(j + 1) * R], in_=stg[0:R, j * 128:(j + 1) * 128],
                                                         identity=ident_f[0:R, 0:R]),
                      r=[r_stg, r_ident], w=[pbr[bank]], sig=(j == ns - 1))
            fw.op("dve", lambda h: h.tensor_copy(out=dst_fm[:, c0 // 128:c0 // 128 + ns, :],
                                                 in_=pb[bank][:, 0:ns * R].rearrange("p (s r) -> p s r", s=ns)),
                  r=[pbr[bank]], w=[r_dst])

    load_rows_fm(prm_norm, 3, D, pn_fm, r_pn, 0)
    load_rows_fm(prm_ssd, 5, XBC, pssd_fm, r_pssd, 1)
    load_rows_fm(prm_cf, 34, D, pcf_fm, r_pcf, 2)
    load_rows_fm(prm_ffn, 4, 2 * FFN, pffn_fm, r_pffn, 3)

    if with_sample:
        smp_f = sb("smp_f", [128, 512]); r_smp = Res("smp")
        Us_f, Bd_f, hA_f, hB_f = smp_f[:, 0:128], smp_f[:, 128:256], smp_f[:, 256:384], smp_f[:, 384:512]
        sel_bc = sb("sel_bc", [128, NSEQ_S, TS], BF16); r_selbc = Res("selbc")
        fw.dma("sp", smp_f[:], c_smp, w=[r_smp])
        fw.dma("pool", sel_bc[:].rearrange("p a b -> p (a b)"), c_selbc.partition_broadcast(128), w=[r_selbc])
        so_t = [sb("so_t%d" % i, [128, 64]) for i in range(2)]; r_so = [Res("so_t%d" % i) for i in range(2)]
        stgo = [sb("stgo%d" % i, [64, 128]) for i in range(2)]; r_stgo = [Res("stgo%d" % i) for i in range(2)]
        stg2 = [stg, sb("stg2_1", [128, 512])]; r_stg2 = [r_stg, Res("stg2_1")]
        spf = {}
        hbuf = sb("hbuf", [128, 4, 2048], BF16)
        h0b = [hbuf[:, i, :].rearrange("p (b n) -> p b n", b=16) for i in range(2)]
        h0T = [hbuf[:, 2 + i, :] for i in range(2)]
        h0f = [hbuf[:, 2 * i:2 * i + 2, :].rearrange("p a f -> p (a f)").bitcast(F32).rearrange("p (b n) -> p b n", b=16) for i in range(2)]
        r_hb = [Res("hb%d" % i) for i in range(4)]
        Cm = [sb("Cm%d" % i, [128, NG, TS], BF16) for i in range(2)]; r_Cm = [Res("Cm%d" % i) for i in range(2)]
        Bm = [sb("Bm%d" % i, [64, 512], BF16) for i in range(2)]; r_Bm = [Res("Bm%d" % i) for i in range(2)]
        dec_fm = sb("dec_fm", [128, NSEQ_S, 16]); r_dec = Res("dec_fm")
        rhs_ab = sb("rhs_ab", [64, 2, NSEQ_S * 16]); r_rab = Res("rhs_ab")
        fw.dma("sp", o_scfc.rearrange("(q r) c -> q r c", r=30)[:, 0:26, :], st_cfc.rearrange("(q r) c -> q r c", r=30)[:, 4:30, :], stream="d2d")

    hst = sb("hst", [128, NG, 512]); r_hst = [Res("hst%d" % g) for g in range(NG)]
    hst_b = sb("hst_b", [128, NG, 512], BF16); r_hstb = [Res("hstb%d" % g) for g in range(NG)]
    halo_ssd = sb("halo_ssd", [128, NSX, 3], BF16); r_halo_ssd = Res("halo_ssd")
    halo_cf = sb("halo_cf", [128, NKD, 30], BF16); r_halo_cf = Res("halo_cf")
    halo_ffn = sb("halo_ffn", [128, 2 * NKF, 2], BF16); r_halo_ffn = Res("halo_ffn")
    st_pssdc = sb("st_pssdc", [128, NSX, 3]); r_st_pssdc = Res("st_pssdc")
    st_pcfc = sb("st_pcfc", [128, NKD, 30]); r_st_pcfc = Res("st_pcfc")
    st_pffc = sb("st_pffc", [128, 2 * NKF, 2]); r_st_pffc = Res("st_pffc")
    for t, rr in [(hst, r_hst), (hst_b, r_hstb)]:
        fw.op("dve", lambda h, t=t: h.memset(t[:].rearrange("p g f -> p (g f)"), 0.0), w=rr)
    for t, rr in [(halo_ssd, r_halo_ssd), (halo_cf, r_halo_cf), (halo_ffn, r_halo_ffn)]:
        fw.op("dve", lambda h, t=t: h.memset(t[:].rearrange("p s f -> p (s f)"), 0.0), w=[rr])

    NSL = 5
    wsl = [sb("wsl%d" % i, [128, 2048], BF16) for i in range(NSL)]
    r_wsl = [Res("wsl%d" % i) for i in range(NSL)]
    units = []

    def add_fm_unit(W, k0, nk, c0):
        units.append(("fm", W[k0 * 128:(k0 + nk) * 128, c0:c0 + 128].rearrange("(k p) c -> p k c", p=128), nk))

    def add_tm_unit(W, k0, nk, c0):
        units.append(("tm", W[k0 * 128:(k0 + nk) * 128, c0:c0 + 512].rearrange("(k p) c -> p k c", p=128), nk))

    for p in passes:
        if p.get("mode") == "prefix":
            for s in range(20):
                add_fm_unit(w_in, 0, 16, C_XBC + 128 * s)
            continue
        for ct in range(4):
            for u in range(4):
                add_tm_unit(w_in, 4 * u, 4, C_Z + 512 * ct)
        for s in range(NSX):
            add_fm_unit(w_in, 0, 16, C_XBC + 128 * s)
        for s in range(NKD):
            add_fm_unit(w_in, 0, 16, C_CFG + 128 * s)
            add_fm_unit(w_in, 0, 16, C_CFA + 128 * s)
        for ct in range(4):
            for u in range(8):
                add_tm_unit(w_out, 4 * u, 4, 512 * ct)
        for (k0, k1) in ((0, 22), (22, NKF)):
            for s in range(k0, k1):
                add_fm_unit(w_up, 0, 16, 128 * s)
                add_fm_unit(w_up, 0, 16, FFN + 128 * s)
            for ct in range(4):
                for ka in range(k0, k1, 4):
                    add_tm_unit(w_down, ka, min(4, k1 - ka), 512 * ct)
    wq = dict(issued=0, used=0)

    def w_prefetch():
        while wq["issued"] < len(units) and wq["issued"] < wq["used"] + NSL:
            i = wq["issued"]
            kind, ap, nk = units[i]
            sl = i % NSL
            if kind == "fm":
                dst = wsl[sl][:, 0:nk * 128].rearrange("p (k c) -> p k c", k=nk)
            else:
                dst = wsl[sl][:, 0:nk * 512].rearrange("p (k c) -> p k c", k=nk)
            fw.dma("pool", dst, ap, w=[r_wsl[sl]], stream="w%d" % sl)
            wq["issued"] += 1

    w_prefetch()

    def w_next(kind):
        w_prefetch()
        i = wq["used"]
        k, ap, nk = units[i]
        assert k == kind, (k, kind, i)
        sl = i % NSL
        cols = 128 if kind == "fm" else 512
        return wsl[sl][:, 0:nk * cols].rearrange("p (k c) -> p k c", k=nk), r_wsl[sl], nk

    def w_done():
        wq["used"] += 1
        w_prefetch()

    xt = [arena[:, o3 + i * 2 * D:o3 + (i + 1) * 2 * D].bitcast(F32) for i in range(2)]; r_xt = [Res("xt%d" % i) for i in range(2)]
    wfin_bc = arena[:, o2:o2 + 2 * D].bitcast(F32)
    xn0 = sb("xn", [128, D], BF16); r_xn0 = Res("xn")
    xn = xn0; r_xn = r_xn0
    junk = xn0; r_junk = r_xn0
    junkA = arena[:, o2:o2 + D]; r_junkA = Res("junkA")
    xnA = arena[:, o2 + D:o2 + 2 * D]; r_xnA = Res("xnA")
    junkD = arena[:, o3:o3 + D]; r_junkD = Res("junkD")
    xnD = arena[:, o3 + D:o3 + 2 * D]; r_xnD = Res("xnD")
    stat_all = sb("stat", [128, 64]); r_stats = [Res("stat%d" % i) for i in range(8)]
    stat_n = [0]

    def stat_set():
        i = stat_n[0] % 8
        stat_n[0] += 1
        return stat_all[:, i * 8:(i + 1) * 8], r_stats[i]
    dt_tok = sb("dt_tok", [128, NSLOT, H]); r_dt = [Res("dt%d" % c) for c in range(NSLOT)]
    pre = [sb("pre%d" % i, [128, PRE_N], BF16) for i in range(2)]
    r_pre = [Res("pre%d" % i) for i in range(2)]
    sg = [sb("sg%d" % i, [128, TT]) for i in range(2)]; r_sg = [Res("sg%d" % i) for i in range(2)]
    dgR = sb("dgR", [128, 4096], BF16)
    diag = [dgR[:, 0:31 * 128].rearrange("p (i c) -> p i c", i=31), sb("diag1", [128, 4, 128], BF16), sb("diag2", [128, 4, 128], BF16)]
    r_diag = [Res("diag%d" % i) for i in range(3)]
    cnt = dict(pre=0, sg=0, diag=0, xt=0, bank=0, stg2=0, so=0, bankcf=0)

    def rot(name, n):
        i = cnt[name] % n
        cnt[name] += 1
        return i

    lnst = sb("lnst", [128, 2, max(TT, 512)]); r_lnst = Res("lnst")
    sm = sb("sm", [128, 5, H]); r_sm = Res("sm")
    dAh = sb("dAh", [128, H], BF16)
    Rg = dgR[:, 0:2048].bitcast(F32); r_Rg = Res("Rg")
    ecs = dgR[:, 2048:4096].bitcast(F32); r_ecs = Res("ecs")
    seg = lnst[:].rearrange("p a t -> p (a t)")[:, 0:1024]; r_seg = r_lnst
    mtcs = sb("mtcs", [128, 4096], BF16)
    MT = [mtcs[:, 0:1024], mtcs[:, 2048:3072]]; r_MT = [Res("MT%d" % i) for i in range(2)]
    Cs = [mtcs[:, 1024:2048], mtcs[:, 3072:4096]]; r_Cs = [Res("Cs%d" % i) for i in range(2)]
    CBTm = [sb("CBTm%d" % i, [128, 128]) for i in range(2)]; r_CBTm = [Res("CBTm%d" % i) for i in range(2)]
    xs_tok = arena[:, o1:o1 + D]; r_xs_tok = Res("xs_tok")
    xdt_tok = arena[:, o1 + D:o1 + 2 * D]; r_xdt = Res("xdt")
    xw_tok = xdt_tok; r_xw = r_xdt
    t1 = arena[:, o1 + 2 * D:o1 + 4 * D].bitcast(F32); r_t1 = Res("t1")
    assert 4 * D <= NKD * TT
    xsD = sb("xsD", [128, 512]); r_xsD = Res("xsD", strict=True)
    B_tok = sb("B_tok", [128, 512], BF16); r_B_tok = Res("B_tok")
    r_xc_ssd = Res("xc_ssd_dummy")

    def norm_to_T(src, rows, r_src, wcol, dstT, r_dst, col0, banks, jnk=None, xnb=None):
        stat, r_stat = stat_set()
        jk, r_jk = jnk if jnk is not None else (junk, r_junk)
        xn, r_xn = xnb if xnb is not None else (xn0, r_xn0)
        fw.op("act", lambda h: h.activation(out=jk[0:rows, :], in_=src, func=AF.Square, accum_out=stat[0:rows, 0:1]),
              r=[r_src], w=[r_jk, r_stat])
        fw.op("act", lambda h: h.activation(out=stat[0:rows, 1:2], in_=stat[0:rows, 0:1], func=AF.Sqrt, scale=1.0 / D, bias=epsb[0:rows, 0:1]),
              r=[r_stat, r_eps], w=[r_stat])
        fw.op("dve", lambda h: h.reciprocal(out=stat[0:rows, 2:3], in_=stat[0:rows, 1:2]), r=[r_stat], w=[r_stat])
        fw.op("dve", lambda h: h.tensor_scalar(out=xn[0:rows, :], in0=src, scalar1=stat[0:rows, 2:3], scalar2=None, op0=ALU.mult),
              r=[r_src, r_stat], w=[r_xn])
        for half in range(2):
            b = banks[half]
            for j in range(8):
                k = half * 8 + j
                fw.op("pe", lambda h, j=j, k=k, b=b: h.transpose(out=pbf(b)[:, j * 128:j * 128 + rows], in_=xn[0:rows, k * 128:(k + 1) * 128],
                                                                   identity=ident_b[0:rows, 0:rows]),
                      r=[r_xn, r_ident], w=[pbr[b]], sig=(j == 7))
            fw.op("dve", lambda h, half=half, b=b: h.tensor_tensor(
                out=dstT[:, half * 8:half * 8 + 8, col0:col0 + rows],
                in0=pbf(b).rearrange("p (k t) -> p k t", k=8)[:, :, 0:rows],
                in1=pn_fm[:, half * 8:half * 8 + 8, wcol:wcol + 1].to_broadcast([128, 8, rows]), op=ALU.mult),
                r=[pbr[b], r_pn], w=r_dst)

    def build_diag(dg, prm_fm, s, ntap, r_prm, r_dg, eng="dve"):
        for i in range(ntap):
            if eng == "dve":
                fw.op("dve", lambda h, i=i: h.tensor_scalar(out=dg[:, i, :], in0=ident_f[:], scalar1=prm_fm[:, s, i:i + 1], scalar2=None, op0=ALU.mult),
                      r=[r_ident, r_prm], w=[r_dg], sig=(i == ntap - 1))
            else:
                fw.op("act", lambda h, i=i: h.activation(out=dg[:, i, :], in_=ident_f[:], func=AF.Identity, scale=prm_fm[:, s, i:i + 1]),
                      r=[r_ident, r_prm], w=[r_dg], sig=(i == ntap - 1))

    def dbg_dump(name, ap, shape, rr):
        if name in dbg:
            dbg_out[name] = nc.dram_tensor("dbg_" + name, list(shape), ap.dtype, kind="ExternalOutput").ap()
            fw.dma("sp", dbg_out[name], ap, r=rr, stream="dbg_" + name)

    for pi, P in enumerate(passes):
        chunks = P["chunks"]
        nch = len(chunks)
        has_s = _has_s(P)
        last = bool(P.get("last"))
        prefix = P.get("mode") == "prefix"
        halo = bool(P.get("halo"))
        yrows = P.get("yrows", chunks)
        xsrc = xpre if prefix else xp
        Tm = nch * 128
        slots = [(c, 128, c * 128) for c in range(nch)]
        if has_s:
            slots.append((nch, TS, Tm))
        slots_out = [sl for sl in slots if not (halo and sl[0] == 0)]
        mtiles = [(c0, min(512, Tm - c0)) for c0 in range(0, Tm, 512)]
        tiles = list(mtiles)
        if has_s:
            tiles.append((Tm, TS))
        assert len(tiles) <= 2
        LT = len(mtiles) - 1
        LN = mtiles[-1][1]

        fw.barrier()
        for (c, rows, col0) in slots:
            xi = rot("xt", 2)
            src_d = xsrc[chunks[c] * 128:(chunks[c] + 1) * 128, :] if c < nch else xsd
            fw.dma("sp", xt[xi][0:rows, :], src_d, w=[r_xt[xi]])
            norm_to_T(xt[xi][0:rows, :], rows, r_xt[xi], 0, hT, [r_hT[c]], col0, (6, 7), jnk=(junkA, r_junkA),
                      xnb=((xnA, r_xnA) if c % 2 else None))
        if pi == 0:
            dbg_dump("hT", hT[:].rearrange("p k t -> p (k t)"), [128, NKD * TT], r_hT)

        for (c, rows, col0) in slots:
            for k in range(NKD):
                fw.op("pe", lambda h, k=k, c=c: h.matmul(pb[5][0:rows, c * 32:(c + 1) * 32], lhsT=hT[:, k, col0:col0 + rows], rhs=wdt[:, k, :],
                                                          start=(k == 0), stop=(k == NKD - 1)),
                      r=[r_hT[c], r_wdt], w=[pbr[5]], sig=(k == NKD - 1))
        for (c, rows, col0) in slots:
            fw.op("dve", lambda h, c=c: h.tensor_tensor(out=dt_tok[0:rows, c, :], in0=pb[5][0:rows, c * 32:(c + 1) * 32], in1=head_bc[0:rows, 0, :], op=ALU.add),
                  r=[pbr[5], r_head], w=[r_dt[c]])
            fw.op("act", lambda h, c=c: h.activation(out=dt_tok[0:rows, c, :], in_=dt_tok[0:rows, c, :], func=AF.Exp), r=[r_dt[c]], w=[r_dt[c]])
            fw.op("act", lambda h, c=c: h.activation(out=dt_tok[0:rows, c, :], in_=dt_tok[0:rows, c, :], func=AF.Ln, bias=1.0), r=[r_dt[c]], w=[r_dt[c]])

        if not prefix:

            for ct in range(4):
                for u in range(4):
                    wv, rw, nk = w_next("tm")
                    for kk in range(nk):
                        k = 4 * u + kk
                        for (c, rows, col0) in slots:
                            fw.op("pe", lambda h, c=c, k=k, kk=kk: h.matmul(pb[c][0:rows, :], lhsT=hT[:, k, col0:col0 + rows], rhs=wv[:, kk, :],
                                                                           start=(k == 0), stop=(k == NKD - 1)),
                                  r=[r_hT[c], rw], w=[pbr[c]], sig=(kk == nk - 1 and c == slots[-1][0]))
                    w_done()
                for (c, rows, col0) in slots:
                    fw.op("act", lambda h, c=c: h.activation(out=zs[0:rows, c, ct * 512:(ct + 1) * 512], in_=pb[c][0:rows, :], func=AF.Silu),
                          r=[pbr[c]], w=[r_zs[c]])

        def fm_gemm(wv, rw, src, r_src, banks):
            for k in range(NKD):
                for ti, (c0, n) in enumerate(tiles):
                    b = banks[ti]
                    fw.op("pe", lambda h, k=k, b=b, c0=c0, n=n: h.matmul(pb[b][:, 0:n], lhsT=wv[:, k, :], rhs=src[:, k, c0:c0 + n],
                                                                        start=(k == 0), stop=(k == NKD - 1)),
                          r=r_src + [rw], w=[pbr[b]], sig=(k == NKD - 1))

        def conv_pe(pr, r_p, dg, r_dg, ntap, hl, banks):
            for ti, (c0, n) in enumerate(tiles):
                b = banks[ti]
                for i in range(ntap):
                    if ti < len(mtiles):
                        rhs = pr[:, c0 + i:c0 + i + n]
                        out = pb[b][:, 0:n]
                    else:
                        base = hl + Tm
                        rhs = pr[:, base:base + NSEQ_S * (hl + 4)].rearrange("p (a b) -> p a b", a=NSEQ_S)[:, :, i:i + 4]
                        out = pb[b][:, 0:TS].rearrange("p (a b) -> p a b", a=NSEQ_S)
                    fw.op("pe", lambda h, i=i, rhs=rhs, out=out: h.matmul(out, lhsT=dg[:, i, :], rhs=rhs, start=(i == 0), stop=(i == ntap - 1)),
                          r=[r_p, r_dg], w=[pbr[b]], sig=(i == ntap - 1))

        SKIND = {"ssd": (st_ssdc, o_sssdc, 3, 1, 3), "cf": (st_cfc, o_scfc, 4, 0, 30), "ffn": (st_ffc, o_sffc, 2, 2, 2)}

        def sample_state_dma(kind, strip):
            state2d, _, _, _, hl = SKIND[kind]
            R_all = NSEQ_S * hl
            ngrp = 4 if R_all > 128 else 1
            rg = R_all // ngrp
            bi = rot("stg2", 2)
            fw.dma("sp", stg2[bi][0:rg, 0:ngrp * 128].rearrange("r (j c) -> r j c", j=ngrp),
                   state2d[:, strip * 128:(strip + 1) * 128].rearrange("(j r) c -> r j c", r=rg), w=[r_stg2[bi]])
            spf[(kind, strip)] = bi

        def sample_in(pr, r_p, kind, strip, gbank, nxt=None, sgb=None, r_sgb=None):
            state2d, out2d, nkk, t0, hl = SKIND[kind]
            R_all = NSEQ_S * hl
            ngrp = 4 if R_all > 128 else 1
            rg = R_all // ngrp
            if (kind, strip) not in spf:
                sample_state_dma(kind, strip)
            bi = spf.pop((kind, strip))
            xbank = 6
            for j in range(ngrp):
                fw.op("pe", lambda h, j=j: h.transpose(out=pb[xbank][:, j * rg:(j + 1) * rg], in_=stg2[bi][0:rg, j * 128:(j + 1) * 128], identity=ident_f[0:rg, 0:rg]),
                      r=[r_stg2[bi], r_ident], w=[pbr[xbank]], sig=(j == ngrp - 1))
            if nxt is not None:
                sample_state_dma(*nxt)
            base = hl + Tm
            pr_s = pr[:, base:base + NSEQ_S * (hl + 4)].rearrange("p (a b) -> p a b", a=NSEQ_S)
            fw.op("act", lambda h: h.copy(out=pr_s[:, :, 0:hl], in_=pb[xbank][:, 0:R_all].rearrange("p (a b) -> p a b", a=NSEQ_S)), r=[pbr[xbank]], w=[r_p])
            g4 = pb[gbank][:, 0:TS].rearrange("p (a b) -> p a b", a=NSEQ_S)
            oi = rot("so", 2)
            so_v = so_t[oi][:, 0:nkk * 16].rearrange("p (t q) -> p q t", t=nkk)
            if kind == "cf":
                sgv = sgb[:, Tm:Tm + TS].rearrange("p (a b) -> p a b", a=NSEQ_S)
                fw.op("dve", lambda h: h.tensor_tensor(out=pr_s[:, :, hl:hl + 4], in0=g4, in1=sgv, op=ALU.mult), r=[pbr[gbank], r_sgb], w=[r_p])
                fw.op("dve", lambda h: h.tensor_tensor(out=so_v, in0=g4, in1=sgv, op=ALU.mult), r=[pbr[gbank], r_sgb], w=[r_so[oi]])
            else:
                fw.op("act", lambda h: h.copy(out=pr_s[:, :, hl:hl + 4], in_=g4), r=[pbr[gbank]], w=[r_p])
                fw.op("dve", lambda h: h.tensor_copy(out=so_v, in_=g4[:, :, t0:4]), r=[pbr[gbank]], w=[r_so[oi]])
            return (kind, strip, oi)

        def sample_out(c):
            kind, strip, oi = c
            state2d, out2d, nkk, t0, hl = SKIND[kind]
            scols = slice(strip * 128, (strip + 1) * 128)
            fw.op("pe", lambda h: h.transpose(out=pb[7][0:nkk * 16, 0:128], in_=so_t[oi][:, 0:nkk * 16], identity=ident_f[:]), r=[r_so[oi], r_ident], w=[pbr[7]])
            fw.op("dve", lambda h: h.tensor_copy(out=stgo[oi][0:nkk * 16, :], in_=pb[7][0:nkk * 16, 0:128]), r=[pbr[7]], w=[r_stgo[oi]])
            o3d = out2d.rearrange("(q r) c -> q r c", r=hl)
            for t in range(nkk):
                fw.dma("sp", o3d[:, hl - nkk + t, scols], stgo[oi][t * 16:(t + 1) * 16, :], r=[r_stgo[oi]])

        def ssd_sample():
            c = nch
            R_ = slice(0, TS)
            scols = slice(Tm, Tm + TS)
            ssd_block(c, TS, Tm, Us_f, True)
            fw.op("act", lambda h: h.activation(out=sm[R_, 3, :], in_=sm[R_, 1, :], func=AF.Exp), r=[r_sm], w=[r_sm])
            fw.op("pe", lambda h: h.matmul(pb[3][R_, 0:H], lhsT=Bd_f[R_, R_], rhs=sm[R_, 0, :], start=True, stop=True), r=[r_smp, r_sm], w=[pbr[3]])
            fw.op("dve", lambda h: h.tensor_tensor(out=sm[R_, 2, :], in0=pb[3][R_, 0:H], in1=sm[R_, 1, :], op=ALU.subtract), r=[pbr[3], r_sm], w=[r_sm])
            fw.op("act", lambda h: h.activation(out=sm[R_, 2, :], in_=sm[R_, 2, :], func=AF.Exp), r=[r_sm], w=[r_sm])
            selv = Bd_f[R_, 0:TS].rearrange("p (q t) -> p q t", t=4)[:, :, 0]
            for ab in range(2):
                dA2 = sm[R_, 0, :].rearrange("p (b two) -> p b two", two=2)[:, :, ab]
                fw.op("dve", lambda h, ab=ab, dA2=dA2: h.tensor_tensor(out=rhs_ab[:, ab, :].rearrange("p (q b) -> p q b", q=NSEQ_S),
                                                                       in0=selv.unsqueeze(2).to_broadcast([TS, NSEQ_S, 16]),
                                                                       in1=dA2.unsqueeze(1).to_broadcast([TS, NSEQ_S, 16]), op=ALU.mult),
                      r=[r_smp, r_sm], w=[r_rab])
            fw.op("pe", lambda h: h.matmul(pb[3][:, 256:512], lhsT=hA_f[R_, :], rhs=rhs_ab[:, 0, :], start=True, stop=False), r=[r_smp, r_rab], w=[pbr[3]], sig=False)
            fw.op("pe", lambda h: h.matmul(pb[3][:, 256:512], lhsT=hB_f[R_, :], rhs=rhs_ab[:, 1, :], start=False, stop=True), r=[r_smp, r_rab], w=[pbr[3]])
            fw.op("act", lambda h: h.activation(out=dec_fm[:].rearrange("p q b -> p (q b)"), in_=pb[3][:, 256:512], func=AF.Exp), r=[pbr[3]], w=[r_dec])
            for q in range(NSEQ_S):
                qi = q % 2
                fw.dma("pool", h0b[qi], st_ssm[q].rearrange("(b m) n -> m b n", m=128), w=[r_hb[qi]])
                for half in range(2):
                    for j in range(8):
                        blk = half * 8 + j
                        fw.op("pe", lambda h, j=j, blk=blk, half=half: h.transpose(out=pbf(4 + half)[:, j * 128:(j + 1) * 128], in_=h0b[qi][:, blk, :], identity=ident_b[:]),
                              r=[r_hb[qi], r_ident], w=[pbr[4 + half]], sig=(j == 7))
                    fw.op("act" if half == 0 else "dve",
                          (lambda h, half=half: h.copy(out=h0T[qi][:, half * 1024:(half + 1) * 1024], in_=pbf(4 + half))) if half == 0 else
                          (lambda h, half=half: h.tensor_copy(out=h0T[qi][:, half * 1024:(half + 1) * 1024], in_=pbf(4 + half))),
                          r=[pbr[4 + half]], w=[r_hb[2 + qi]])
                fw.op("dve", lambda h: h.tensor_tensor(out=Cm[qi][:], in0=xc[:, 20:24, scols], in1=sel_bc[:, q, :].unsqueeze(1).to_broadcast([128, NG, TS]), op=ALU.mult),
                      r=r_xc[20:24] + [r_selbc], w=[r_Cm[qi]])
                for g in range(NG):
                    fw.op("pe", lambda h, g=g: h.matmul(pb[g][R_, :], lhsT=Cm[qi][:, g, :], rhs=h0T[qi][:, g * 512:(g + 1) * 512], start=(q == 0), stop=(q == NSEQ_S - 1)),
                          r=[r_Cm[qi], r_hb[2 + qi]], w=[pbr[g]], sig=(g == NG - 1))
            for g in range(NG):
                fw.op("dve", lambda h, g=g: h.tensor_tensor(out=xsD[R_, :].rearrange("p (h q) -> p h q", h=8), in0=pb[g][R_, :].rearrange("p (h q) -> p h q", h=8),
                                                            in1=sm[R_, 3, g * 8:(g + 1) * 8].unsqueeze(2).to_broadcast([TS, 8, HP]), op=ALU.mult),
                      r=[pbr[g], r_sm], w=[r_xsD])
                fw.op("dve", lambda h, g=g: h.tensor_tensor(out=t1[R_, g * 512:(g + 1) * 512], in0=t1[R_, g * 512:(g + 1) * 512], in1=xsD[R_, :], op=ALU.add),
                      r=[r_xsD, r_t1], w=[r_t1])
            fw.op("dve", lambda h: h.tensor_tensor(out=xw_tok[R_, :].rearrange("p (h q) -> p h q", h=H), in0=xdt_tok[R_, :].rearrange("p (h q) -> p h q", h=H),
                                                   in1=sm[R_, 2, :].unsqueeze(2).to_broadcast([TS, H, HP]), op=ALU.mult), r=[r_xdt, r_sm], w=[r_xw])
            for q in range(NSEQ_S):
                qi = q % 2
                rr = [r_hb[2 * qi], r_hb[2 * qi + 1]]
                fw.dma("act", h0f[qi], st_ssm[q].rearrange("(b m) n -> m b n", m=128), w=rr, stream="h0f%d" % qi)
                fw.op("dve", lambda h: h.tensor_scalar(out=Bm[qi][:], in0=B_tok[R_, :], scalar1=selv[:, q:q + 1], scalar2=None, op0=ALU.mult),
                      r=[r_B_tok, r_smp], w=[r_Bm[qi]])
                for blk in range(16):
                    bk = blk // 4
                    fw.op("pe", lambda h, blk=blk, bk=bk: h.matmul(pb[bk][:, (blk % 4) * 128:(blk % 4 + 1) * 128], lhsT=xw_tok[R_, blk * 128:(blk + 1) * 128],
                                                                  rhs=Bm[qi][:, bk * 128:(bk + 1) * 128], start=True, stop=True),
                          r=[r_xw, r_Bm[qi]], w=[pbr[bk]], sig=(blk % 4 == 3))
                for blk in range(16):
                    bk = blk // 4
                    fw.op("dve", lambda h, blk=blk, bk=bk: h.scalar_tensor_tensor(out=h0f[qi][:, blk, :], in0=h0f[qi][:, blk, :], scalar=dec_fm[:, q, blk:blk + 1],
                                                                                 in1=pb[bk][:, (blk % 4) * 128:(blk % 4 + 1) * 128], op0=ALU.mult, op1=ALU.add),
                          r=rr + [r_dec, pbr[bk]], w=rr, sig=(blk % 4 == 3))
                fw.dma("sp", o_sssm[q].rearrange("(b m) n -> m b n", m=128), h0f[qi], r=rr, stream="h0f%d" % qi)
            fw.op("dve", lambda h: h.tensor_tensor(out=t1[R_, :], in0=t1[R_, :], in1=zs[R_, c, :], op=ALU.mult), r=[r_t1, r_zs[c]], w=[r_t1])
            norm_to_T(t1[R_, :], TS, r_t1, 2, xc, r_xc[0:NKD], Tm, (4, 5))

        bank_pairs = [(0, 1), (2, 3), (4, 5), (6, 7)]
        NBP = 3 if has_s else 4
        NBPcf = 2
        def pipelined(items, stage1, diag_fn, stage2):
            prev = None
            for it in items:
                if prev is not None:
                    diag_fn(prev)
                ctx = stage1(it)
                if prev is not None:
                    stage2(prev)
                prev = ctx
            if prev is not None:
                diag_fn(prev)
                stage2(prev)

        def xbc_s1(s):
            wv, rw, nk = w_next("fm")
            bp = bank_pairs[rot("bank", NBP)]
            fm_gemm(wv, rw, hT, r_hT[:len(slots)], bp)
            w_done()
            pi_ = rot("pre", 2)
            pr = pre[pi_]
            fw.op("dve", lambda h: h.tensor_copy(out=pr[:, 0:3], in_=halo_ssd[:, s, :]), r=[r_halo_ssd], w=[r_pre[pi_]])
            for ti, (c0, n) in enumerate(mtiles):
                fw.op("act", lambda h, ti=ti, c0=c0, n=n: h.copy(out=pr[:, 3 + c0:3 + c0 + n], in_=pb[bp[ti]][:, 0:n]), r=[pbr[bp[ti]]], w=[r_pre[pi_]])
            fw.op("dve", lambda h: h.tensor_copy(out=halo_ssd[:, s, :], in_=pr[:, Tm:Tm + 3]), r=[r_pre[pi_]], w=[r_halo_ssd])
            if last:
                fw.op("dve", lambda h: h.tensor_copy(out=st_pssdc[:, s, :], in_=pb[bp[LT]][:, LN - 3:LN]), r=[pbr[bp[LT]]], w=[r_st_pssdc])
            bp2 = bank_pairs[rot("bank", NBP)]
            so = None
            if has_s:
                so = sample_in(pr, r_pre[pi_], "ssd", s, bp[1], nxt=("ssd", s + 1) if s + 1 < NSX else None)
            return dict(s=s, pi=pi_, bp2=bp2, di=1 + rot("diag", 2), so=so)

        def xbc_diag(c):
            build_diag(diag[c["di"]], pssd_fm, c["s"], 4, r_pssd, r_diag[c["di"]])

        def xbc_s2(c):
            s_, bp2, di = c["s"], c["bp2"], c["di"]
            if c["so"] is not None:
                sample_out(c["so"])
            conv_pe(pre[c["pi"]], r_pre[c["pi"]], diag[di], r_diag[di], 4, 3, bp2)
            for ti, (c0, n) in enumerate(tiles):
                fw.op("act", lambda h, ti=ti, c0=c0, n=n: h.activation(out=xc[:, s_, c0:c0 + n], in_=pb[bp2[ti]][:, 0:n], func=AF.Silu,
                                                                     bias=pssd_fm[:, s_, 4:5], scale=1.0),
                      r=[pbr[bp2[ti]], r_pssd], w=[r_xc[s_]])

        pipelined(range(NSX if not prefix else 20), xbc_s1, xbc_diag, xbc_s2)
        if pi == 0:
            dbg_dump("xc", xc[:].rearrange("p k t -> p (k t)"), [128, NSX * TT], r_xc)

        def cf_s1(s):
            si = rot("sg", 2)
            pi_ = rot("pre", 2)
            pr = pre[pi_]
            wv, rw, nk = w_next("fm")
            bpg = bank_pairs[rot("bankcf", NBPcf)]
            fm_gemm(wv, rw, hT, r_hT[:len(slots)], bpg)
            w_done()
            for ti, (c0, n) in enumerate(tiles):
                fw.op("act", lambda h, ti=ti, c0=c0, n=n: h.activation(out=sg[si][:, c0:c0 + n], in_=pb[bpg[ti]][:, 0:n], func=AF.Sigmoid),
                      r=[pbr[bpg[ti]]], w=[r_sg[si]])
            wv, rw, nk = w_next("fm")
            bpa = bank_pairs[rot("bankcf", NBPcf)]
            fm_gemm(wv, rw, hT, r_hT[:len(slots)], bpa)
            w_done()
            fw.op("dve", lambda h: h.tensor_copy(out=pr[:, 0:30], in_=halo_cf[:, s, :]), r=[r_halo_cf], w=[r_pre[pi_]])
            for ti, (c0, n) in enumerate(mtiles):
                fw.op("dve", lambda h, ti=ti, c0=c0, n=n: h.tensor_tensor(out=pr[:, 30 + c0:30 + c0 + n], in0=pb[bpa[ti]][:, 0:n], in1=sg[si][:, c0:c0 + n], op=ALU.mult),
                      r=[pbr[bpa[ti]], r_sg[si]], w=[r_pre[pi_]])
            fw.op("dve", lambda h: h.tensor_copy(out=halo_cf[:, s, :], in_=pr[:, Tm:Tm + 30]), r=[r_pre[pi_]], w=[r_halo_cf])
            if last:
                fw.op("dve", lambda h: h.tensor_tensor(out=st_pcfc[:, s, :], in0=pb[bpa[LT]][:, LN - 30:LN], in1=sg[si][:, Tm - 30:Tm], op=ALU.mult),
                      r=[pbr[bpa[LT]], r_sg[si]], w=[r_st_pcfc])
            bp2 = bank_pairs[rot("bankcf", NBPcf)]
            so = None
            if has_s:
                so = sample_in(pr, r_pre[pi_], "cf", s, bpa[1], nxt=("cf", s + 1) if s + 1 < NKD else None, sgb=sg[si], r_sgb=r_sg[si])
            return dict(s=s, pi=pi_, bp2=bp2, so=so)

        def cf_diag(c):
            build_diag(diag[0], pcf_fm, c["s"], 31, r_pcf, r_diag[0], eng="act")

        def cf_s2(c):
            s_, bp2 = c["s"], c["bp2"]
            if c["so"] is not None:
                sample_out(c["so"])
            conv_pe(pre[c["pi"]], r_pre[c["pi"]], diag[0], r_diag[0], 31, 30, bp2)
            for ti, (c0, n) in enumerate(tiles):
                fw.op("act", lambda h, ti=ti, c0=c0, n=n: h.activation(out=mixcf[:, s_, c0:c0 + n], in_=pb[bp2[ti]][:, 0:n], func=AF.Identity,
                                                                     bias=pcf_fm[:, s_, 31:32], scale=1.0),
                      r=[pbr[bp2[ti]], r_pcf], w=[r_mixcf[s_]])

        def cf_layernorm():
            for ti, (c0, n) in enumerate(tiles):
                b1, b2 = (0, 1) if ti == 0 else (2, 3)
                for k in range(NKD):
                    sq = pre[k % 2]
                    fw.op("act", lambda h, k=k, sq=sq: h.activation(out=sq[:, 0:n], in_=mixcf[:, k, c0:c0 + n], func=AF.Square),
                          r=[r_mixcf[k]], w=[r_pre[k % 2]])
                    fw.op("pe", lambda h, k=k: h.matmul(pb[b1][:, 0:n], lhsT=ones_b[:], rhs=mixcf[:, k, c0:c0 + n], start=(k == 0), stop=(k == NKD - 1)),
                          r=[r_mixcf[k], r_ones], w=[pbr[b1]], sig=(k == NKD - 1))
                    fw.op("pe", lambda h, k=k, sq=sq: h.matmul(pb[b2][:, 0:n], lhsT=ones_b[:], rhs=sq[:, 0:n], start=(k == 0), stop=(k == NKD - 1)),
                          r=[r_pre[k % 2], r_ones], w=[pbr[b2]], sig=True)
                mean_bc = lnst[:, 0, c0:c0 + n]
                rstd_bc = lnst[:, 1, c0:c0 + n]
                fw.op("dve", lambda h: h.tensor_scalar(out=mean_bc, in0=pb[b1][:, 0:n], scalar1=1.0 / D, scalar2=None, op0=ALU.mult), r=[pbr[b1]], w=[r_lnst])
                fw.op("dve", lambda h: h.tensor_tensor(out=rstd_bc, in0=mean_bc, in1=mean_bc, op=ALU.mult), r=[r_lnst], w=[r_lnst])
                fw.op("dve", lambda h: h.scalar_tensor_tensor(out=rstd_bc, in0=pb[b2][:, 0:n], scalar=1.0 / D, in1=rstd_bc, op0=ALU.mult, op1=ALU.subtract),
                      r=[pbr[b2], r_lnst], w=[r_lnst])
                fw.op("act", lambda h: h.activation(out=rstd_bc, in_=rstd_bc, func=AF.Sqrt, bias=epsb[:, 0:1], scale=1.0), r=[r_lnst, r_eps], w=[r_lnst])
                fw.op("dve", lambda h: h.reciprocal(out=rstd_bc, in_=rstd_bc), r=[r_lnst], w=[r_lnst])
                for k in range(NKD):
                    tl = sg[k % 2]
                    fw.op("dve", lambda h, k=k, tl=tl: h.tensor_tensor(out=tl[:, 0:n], in0=mixcf[:, k, c0:c0 + n], in1=mean_bc, op=ALU.subtract),
                          r=[r_mixcf[k], r_lnst], w=[r_sg[k % 2]])
                    fw.op("dve", lambda h, k=k, tl=tl: h.tensor_tensor(out=tl[:, 0:n], in0=tl[:, 0:n], in1=rstd_bc, op=ALU.mult),
                          r=[r_sg[k % 2], r_lnst], w=[r_sg[k % 2]])
                    fw.op("act", lambda h, k=k, tl=tl: h.activation(out=mixcf[:, k, c0:c0 + n], in_=tl[:, 0:n], func=AF.Silu,
                                                                   scale=pcf_fm[:, k, 32:33], bias=pcf_fm[:, k, 33:34]),
                          r=[r_sg[k % 2], r_pcf], w=[r_mixcf[k]])
            if pi == 0:
                dbg_dump("mixcf", mixcf[:].rearrange("p k t -> p (k t)"), [128, NKD * TT], r_mixcf)

        cfgB = dict(xs=xs_tok, xdt=xdt_tok, t1=t1, r_xs=r_xs_tok, r_xdt=r_xdt, r_t1=r_t1,
                    R=[Rg[:, 0:512], Rg[:, 512:1024]], r_R=[r_Rg, r_Rg], ecs=ecs, r_ecs=r_ecs, nmt=2,
                    b_xs=(0, 1), b_B=2, b_misc=3, b_cs=(4, 5), b_y=(6, 7), b_nt=(0, 1))
        if with_sample:
            cfgM = dict(xs=hbuf[:, 0, :], xdt=hbuf[:, 1, :], t1=hbuf[:, 2:4, :].rearrange("p a f -> p (a f)").bitcast(F32),
                        r_xs=r_hb[0], r_xdt=r_hb[1], r_t1=r_hb[2],
                        R=[xsD[:, :], xsD[:, :]], r_R=[r_xsD, r_xsD], ecs=mtcs[:, 2048:4096].bitcast(F32), r_ecs=Res("ecsM"), nmt=1,
                        b_xs=(4, 5), b_B=7, b_misc=7, b_cs=(4, 5), b_y=(6, 6), b_nt=(4, 5))

        def ssd_block(c, rows, col0, Um, sample, cf=None, hooks=None):
            cf = cf or cfgB
            xs_tok, xdt_tok, t1 = cf["xs"], cf["xdt"], cf["t1"]
            xw_tok = xdt_tok
            r_xs_tok, r_xdt, r_t1 = cf["r_xs"], cf["r_xdt"], cf["r_t1"]
            r_xw = r_xdt
            ecs, r_ecs = cf["ecs"], cf["r_ecs"]
            bxs, bB, bm, bcs, bnt = cf["b_xs"], cf["b_B"], cf["b_misc"], cf["b_cs"], cf["b_nt"]
            cols = slice(col0, col0 + rows)
            R_ = slice(0, rows)
            for half in range(2):
                for j in range(8):
                    k = half * 8 + j
                    fw.op("pe", lambda h, j=j, k=k, half=half: h.transpose(out=pbf(bxs[half])[R_, j * 128:(j + 1) * 128], in_=xc[:, k, cols], identity=ident_b[:]),
                          r=[r_xc[k], r_ident], w=[pbr[bxs[half]]], sig=(j == 7))
                fw.op("act", lambda h, half=half: h.copy(out=xs_tok[R_, half * 1024:(half + 1) * 1024], in_=pbf(bxs[half])[R_, :]), r=[pbr[bxs[half]]], w=[r_xs_tok])
            for g in range(NG):
                fw.op("pe", lambda h, g=g: h.transpose(out=pbf(bB)[R_, g * 128:(g + 1) * 128], in_=xc[:, 16 + g, cols], identity=ident_b[:]),
                      r=[r_xc[16 + g], r_ident], w=[pbr[bB]], sig=(g == NG - 1))
            fw.op("dve", lambda h: h.tensor_copy(out=B_tok[R_, :], in_=pbf(bB)[R_, 0:512]), r=[pbr[bB]], w=[r_B_tok])
            dtc = dt_tok[R_, c, :]
            fw.op("dve", lambda h: h.tensor_tensor(out=sm[R_, 0, :], in0=dtc, in1=head_bc[R_, 1, :], op=ALU.mult), r=[r_dt[c], r_head], w=[r_sm])
            fw.op("dve", lambda h: h.tensor_tensor(out=xdt_tok[R_, :].rearrange("p (h q) -> p h q", h=H), in0=xs_tok[R_, :].rearrange("p (h q) -> p h q", h=H),
                                                   in1=dtc.unsqueeze(2).to_broadcast([rows, H, HP]), op=ALU.mult), r=[r_xs_tok, r_dt[c]], w=[r_xdt])
            fw.op("pe", lambda h: h.matmul(pb[bm][R_, 0:H], lhsT=Um[R_, R_], rhs=sm[R_, 0, :], start=True, stop=True), r=[r_U, r_sm], w=[pbr[bm]])
            if not prefix:
                fw.op("dve", lambda h: h.tensor_copy(out=dAh[R_, :], in_=sm[R_, 0, :]), r=[r_sm], w=[r_sm])
                fw.op("dve", lambda h: h.tensor_tensor(out=sm[R_, 4, :], in0=sm[R_, 0, :], in1=dAh[R_, :], op=ALU.subtract), r=[r_sm], w=[r_sm])
            fw.op("dve", lambda h: h.tensor_copy(out=sm[R_, 1, :], in_=pb[bm][R_, 0:H]), r=[pbr[bm]], w=[r_sm])
            if prefix:
                fw.op("pe", lambda h: h.matmul(pb[bm][:, 64:64 + H], lhsT=ones_f[:], rhs=sm[:, 0, :], start=True, stop=True), r=[r_ones, r_sm], w=[pbr[bm]])
                fw.op("dve", lambda h: h.tensor_copy(out=sm[:, 2, :], in_=pb[bm][:, 64:64 + H]), r=[pbr[bm]], w=[r_sm])
                fw.op("act", lambda h: h.activation(out=sm[:, 3, :], in_=pb[bm][:, 64:64 + H], func=AF.Exp), r=[pbr[bm]], w=[r_sm])
            v8 = lambda ap: ap.rearrange("p (a b) -> p a b", a=8)[:, :, R_]
            v4 = lambda ap: ap.rearrange("p (a b) -> p a b", a=4)[:, :, R_]

            def p1(g):
                for half in range(2):
                    Rh, r_Rh = cf["R"][half], cf["r_R"][half]
                    hh = slice(g * 8 + half * 4, g * 8 + half * 4 + 4)
                    Rb = Rh.bitcast(BF16)
                    for part, src in ((0, dAh[R_, hh]), (1, sm[R_, 4, hh])):
                        fw.op("dve", lambda h, part=part, src=src, Rb=Rb: h.tensor_tensor(out=Rb[R_, part * 512:(part + 1) * 512].rearrange("p (a b) -> p a b", a=4),
                                                                                     in0=src.unsqueeze(2).to_broadcast([rows, 4, 128]),
                                                                                     in1=Um[R_, :].unsqueeze(1).to_broadcast([rows, 4, 128]), op=ALU.mult),
                              r=[r_sm, r_U], w=[r_Rh])
                    for part in range(2):
                        fw.op("pe", lambda h, half=half, part=part, Rb=Rb: h.matmul(pb[bcs[half]][R_, :], lhsT=ones_b[R_, R_], rhs=Rb[R_, part * 512:(part + 1) * 512],
                                                                                 start=(part == 0), stop=(part == 1)),
                              r=[r_ones, r_Rh], w=[pbr[bcs[half]]], sig=(part == 1))
                fw.op("pe", lambda h, g=g: h.matmul(pb[bm][R_, 128:128 + rows], lhsT=xc[:, 16 + g, cols], rhs=xc[:, 20 + g, cols], start=True, stop=True),
                      r=[r_xc[16 + g], r_xc[20 + g]], w=[pbr[bm]])

            if not prefix:
                p1(0)
            for g in range(NG if not prefix else 0):
                gi = g % cf["nmt"]
                hs = slice(g * 8, (g + 1) * 8)
                if hooks:
                    hooks[0]()
                for half in range(2):
                    hh = slice(g * 8 + half * 4, g * 8 + half * 4 + 4)
                    bc = bcs[half]
                    fw.op("act", lambda h, half=half, bc=bc: h.activation(out=v4(ecs[R_, half * 512:(half + 1) * 512]), in_=v4(pb[bc][R_, :]), func=AF.Exp),
                          r=[pbr[bc]], w=[r_ecs])
                    for q4 in range(4):
                        hd = g * 8 + half * 4 + q4
                        o0 = half * 512 + q4 * 128
                        fw.op("dve", lambda h, bc=bc, q4=q4, hd=hd, o0=o0: h.tensor_scalar(out=seg[R_, o0:o0 + rows], in0=pb[bc][R_, q4 * 128:q4 * 128 + rows],
                                                                                        scalar1=sm[R_, 1, hd:hd + 1], scalar2=0.0, op0=ALU.subtract, op1=ALU.min),
                              r=[pbr[bc], r_sm], w=[r_seg], sig=(q4 == 3))
                    if not sample:
                        fw.op("dve", lambda h, bc=bc, hh=hh: h.tensor_copy(out=sm[R_, 2, hh], in_=pb[bc][R_, :].rearrange("p (a b) -> p a b", a=4)[:, :, rows - 1]),
                              r=[pbr[bc]], w=[r_sm])
                if not sample:
                    fw.op("dve", lambda h, hs=hs: h.tensor_copy(out=sm[R_, 3, hs], in_=ecs[R_, :].rearrange("p (a b) -> p a b", a=8)[:, :, rows - 1]), r=[r_ecs], w=[r_sm])
                fw.op("act", lambda h: h.activation(out=v8(seg[R_, :]), in_=v8(seg[R_, :]), func=AF.Exp), r=[r_seg], w=[r_seg])
                fw.op("dve", lambda h, gi=gi: h.tensor_tensor(out=CBTm[gi][R_, R_], in0=pb[bm][R_, 128:128 + rows], in1=Um[R_, R_], op=ALU.mult), r=[pbr[bm], r_U], w=[r_CBTm[gi]])
                fw.op("dve", lambda h, gi=gi: h.scalar_tensor_tensor(out=v8(MT[gi][R_, :]), in0=v8(seg[R_, :]),
                                                                     scalar=1.0, in1=CBTm[gi][R_, R_].unsqueeze(1).to_broadcast([rows, 8, rows]), op0=ALU.min, op1=ALU.mult),
                      r=[r_seg, r_CBTm[gi]], w=[r_MT[gi]])
                if not sample:
                    fw.op("dve", lambda h, gi=gi, g=g: h.tensor_tensor(out=Cs[gi][:].rearrange("p (a b) -> p a b", a=8), in0=ecs[:].rearrange("p (a b) -> p a b", a=8),
                                                                       in1=xc[:, 20 + g, cols].unsqueeze(1).to_broadcast([128, 8, 128]), op=ALU.mult),
                          r=[r_ecs, r_xc[20 + g]], w=[r_Cs[gi]])
                if hooks:
                    hooks[1]()
                if g + 1 < NG:
                    p1(g + 1)
                yb = cf["b_y"][g % 2]
                for hh in range(8):
                    hd = g * 8 + hh
                    fw.op("pe", lambda h, hh=hh, hd=hd, gi=gi, yb=yb: h.matmul(pb[yb][R_, hh * 64:(hh + 1) * 64], lhsT=MT[gi][R_, hh * 128:hh * 128 + rows],
                                                                           rhs=xdt_tok[R_, hd * 64:(hd + 1) * 64], start=True, stop=sample),
                          r=[r_MT[gi], r_xdt], w=[pbr[yb]], sig=(sample and hh == 7))
                    if not sample:
                        fw.op("pe", lambda h, hh=hh, g=g, gi=gi, yb=yb: h.matmul(pb[yb][:, hh * 64:(hh + 1) * 64], lhsT=Cs[gi][:, hh * 128:(hh + 1) * 128],
                                                                             rhs=hst_b[:, g, hh * 64:(hh + 1) * 64], start=False, stop=True),
                              r=[r_Cs[gi], r_hstb[g]], w=[pbr[yb]], sig=(hh == 7))
                fw.op("dve", lambda h, g=g, hs=hs: h.tensor_tensor(out=xsD[R_, :].rearrange("p (h q) -> p h q", h=8), in0=xs_tok[R_, g * 512:(g + 1) * 512].rearrange("p (h q) -> p h q", h=8),
                                                                   in1=head_bc[R_, 2, hs].unsqueeze(2).to_broadcast([rows, 8, HP]), op=ALU.mult), r=[r_xs_tok, r_head], w=[r_xsD])
                fw.op("dve", lambda h, g=g, yb=yb: h.tensor_tensor(out=t1[R_, g * 512:(g + 1) * 512], in0=pb[yb][R_, :], in1=xsD[R_, :], op=ALU.add),
                      r=[pbr[yb], r_xsD], w=[r_t1])
            if sample:
                return
            fw.op("dve", lambda h: h.tensor_tensor(out=sm[:, 2, :], in0=sm[:, 2, :], in1=sm[:, 1, :], op=ALU.subtract), r=[r_sm], w=[r_sm])
            fw.op("act", lambda h: h.activation(out=sm[:, 2, :], in_=sm[:, 2, :], func=AF.Exp), r=[r_sm], w=[r_sm])
            fw.op("dve", lambda h: h.tensor_tensor(out=xw_tok[:].rearrange("p (h q) -> p h q", h=H), in0=xdt_tok[:].rearrange("p (h q) -> p h q", h=H),
                                                   in1=sm[:, 2, :].unsqueeze(2).to_broadcast([128, H, HP]), op=ALU.mult), r=[r_xdt, r_sm], w=[r_xw])
            for g in range(NG):
                yb = bcs[g % 2]
                fw.op("pe", lambda h, g=g, yb=yb: h.matmul(pb[yb][:, :], lhsT=B_tok[:, g * 128:(g + 1) * 128], rhs=xw_tok[:, g * 512:(g + 1) * 512], start=True, stop=True),
                      r=[r_B_tok, r_xw], w=[pbr[yb]])
                fw.op("dve", lambda h, g=g: h.tensor_tensor(out=hst[:, g, :].rearrange("p (a b) -> p a b", a=8), in0=hst[:, g, :].rearrange("p (a b) -> p a b", a=8),
                                                            in1=sm[:, 3, g * 8:(g + 1) * 8].unsqueeze(2).to_broadcast([128, 8, HP]), op=ALU.mult),
                      r=[r_hst[g], r_sm], w=[r_hst[g]])
                fw.op("dve", lambda h, g=g, yb=yb: h.tensor_tensor(out=hst[:, g, :], in0=hst[:, g, :], in1=pb[yb][:, :], op=ALU.add), r=[r_hst[g], pbr[yb]], w=[r_hst[g]])
                if halo and c == 0:
                    fw.op("dve", lambda h, g=g: h.tensor_scalar(out=hst[:, g, :], in0=hst[:, g, :], scalar1=flg[:, 0:1], scalar2=None, op0=ALU.mult),
                          r=[r_hst[g], r_flg], w=[r_hst[g]])
                fw.op("act", lambda h, g=g: h.copy(out=hst_b[:, g, :], in_=hst[:, g, :]), r=[r_hst[g]], w=[r_hstb[g]])
            if prefix:
                return
            fw.op("dve", lambda h: h.tensor_tensor(out=t1[:], in0=t1[:], in1=zs[:, c, :], op=ALU.mult), r=[r_t1, r_zs[c]], w=[r_t1])
            norm_to_T(t1[:], 128, r_t1, 2, xc, r_xc[0:NKD], col0, bnt)

        if prefix:
            fw.barrier()
            for c in range(nch):
                ssd_block(c, 128, c * 128, U_f, False)
            continue
        cfst = dict(idx=0, prev=None, cur=None, did=False)

        def cf_hook_a():
            cfst["did"] = cfst["idx"] < NKD
            if cfst["did"]:
                if cfst["prev"] is not None:
                    cf_diag(cfst["prev"])
                cfst["cur"] = cf_s1(cfst["idx"])
                cfst["idx"] += 1

        def cf_hook_b():
            if cfst["did"]:
                if cfst["prev"] is not None:
                    cf_s2(cfst["prev"])
                cfst["prev"] = cfst["cur"]

        for c in range(nch):
            ssd_block(c, 128, c * 128, U_f, False, cf=cfgM, hooks=(cf_hook_a, cf_hook_b))
        while cfst["idx"] < NKD:
            cf_hook_a()
            cf_hook_b()
        if cfst["prev"] is not None:
            cf_diag(cfst["prev"])
            cf_s2(cfst["prev"])
        cf_layernorm()
        fw.barrier()
        if has_s:
            ssd_sample()
        if pi == 0:
            dbg_dump("mixssd", xc[:, 0:NKD, :].rearrange("p k t -> p (k t)"), [128, NKD * TT], r_xc)
        if last:
            dbg_dump("hst", hst[:].rearrange("p g f -> p (g f)"), [128, NG * 512], r_hst)

        fw.barrier()
        for (c, rows, col0) in slots:
            src_d = xp[chunks[c] * 128:(chunks[c] + 1) * 128, :] if c < nch else xsd
            fw.dma("sp", x1[0:rows, c, :], src_d, w=[r_x1[c]])
        for ct in range(4):
            for u in range(8):
                wv, rw, nk = w_next("tm")
                for kk in range(nk):
                    k = 4 * u + kk
                    for (c, rows, col0) in slots:
                        lhsT = xc[:, k, col0:col0 + rows] if k < NKD else mixcf[:, k - NKD, col0:col0 + rows]
                        rr = r_xc[k] if k < NKD else r_mixcf[k - NKD]
                        fw.op("pe", lambda h, c=c, k=k, kk=kk, lhsT=lhsT, rows=rows: h.matmul(pb[c][0:rows, :], lhsT=lhsT, rhs=wv[:, kk, :],
                                                                                           start=(k == 0), stop=(k == 2 * NKD - 1)),
                              r=[rr, rw], w=[pbr[c]], sig=(kk == nk - 1 and c == slots[-1][0]))
                w_done()
            for (c, rows, col0) in slots:
                fw.op("dve", lambda h, c=c, rows=rows: h.tensor_tensor(out=x1[0:rows, c, ct * 512:(ct + 1) * 512], in0=pb[c][0:rows, :],
                                                                      in1=x1[0:rows, c, ct * 512:(ct + 1) * 512], op=ALU.add),
                      r=[pbr[c], r_x1[c]], w=[r_x1[c]])
        if pi == 0:
            dbg_dump("x1", x1[:].rearrange("p c f -> p (c f)"), [128, NSLOT * D], r_x1)

        fw.barrier()
        for (c, rows, col0) in slots:
            norm_to_T(x1[0:rows, c, :], rows, r_x1[c], 1, h2T, [r_h2T[c]], col0, (6, 7), jnk=(junkD, r_junkD),
                      xnb=((xnD, r_xnD) if c % 2 else None))
        for (k0, k1) in ((0, KH), (KH, NKF)):
            def ffn_s1(item):
                s, which, si = item[0:3]
                strip = s + which * NKF
                wv, rw, nk = w_next("fm")
                bp = bank_pairs[rot("bank", NBP)]
                fm_gemm(wv, rw, h2T, r_h2T[:len(slots)], bp)
                w_done()
                pi_ = rot("pre", 2)
                pr = pre[pi_]
                fw.op("dve", lambda h: h.tensor_copy(out=pr[:, 0:2], in_=halo_ffn[:, strip, :]), r=[r_halo_ffn], w=[r_pre[pi_]])
                for ti, (c0, n) in enumerate(mtiles):
                    fw.op("act", lambda h, ti=ti, c0=c0, n=n: h.copy(out=pr[:, 2 + c0:2 + c0 + n], in_=pb[bp[ti]][:, 0:n]), r=[pbr[bp[ti]]], w=[r_pre[pi_]])
                if halo:
                    fw.op("dve", lambda h: h.tensor_scalar(out=pr[:, 2:2 + 128], in0=pr[:, 2:2 + 128], scalar1=flg[:, 0:1], scalar2=None, op0=ALU.mult),
                          r=[r_pre[pi_], r_flg], w=[r_pre[pi_]])
                fw.op("dve", lambda h: h.tensor_copy(out=halo_ffn[:, strip, :], in_=pr[:, Tm:Tm + 2]), r=[r_pre[pi_]], w=[r_halo_ffn])
                if last:
                    fw.op("dve", lambda h: h.tensor_copy(out=st_pffc[:, strip, :], in_=pb[bp[LT]][:, LN - 2:LN]), r=[pbr[bp[LT]]], w=[r_st_pffc])
                bp2 = bank_pairs[rot("bank", NBP)]
                so = None
                if has_s:
                    so = sample_in(pr, r_pre[pi_], "ffn", strip, bp[1], nxt=item[3] if len(item) > 3 else None)
                return dict(s=s, which=which, si=si, strip=strip, pi=pi_, bp2=bp2, di=1 + rot("diag", 2), so=so)

            def ffn_diag(c):
                build_diag(diag[c["di"]], pffn_fm, c["strip"], 3, r_pffn, r_diag[c["di"]])

            def ffn_s2(c):
                s_, which, si, strip, bp2, di = c["s"], c["which"], c["si"], c["strip"], c["bp2"], c["di"]
                if c["so"] is not None:
                    sample_out(c["so"])
                conv_pe(pre[c["pi"]], r_pre[c["pi"]], diag[di], r_diag[di], 3, 2, bp2)
                for ti, (c0, n) in enumerate(tiles):
                    if which == 0:
                        fw.op("act", lambda h, ti=ti, c0=c0, n=n: h.activation(out=sg[si][:, c0:c0 + n], in_=pb[bp2[ti]][:, 0:n], func=AF.Silu,
                                                                             bias=pffn_fm[:, strip, 3:4], scale=1.0),
                              r=[pbr[bp2[ti]], r_pffn], w=[r_sg[si]])
                    else:
                        fw.op("dve", lambda h, ti=ti, c0=c0, n=n: h.scalar_tensor_tensor(out=gv[:, s_ - k0, c0:c0 + n], in0=pb[bp2[ti]][:, 0:n],
                                                                                       scalar=pffn_fm[:, strip, 3:4], in1=sg[si][:, c0:c0 + n],
                                                                                       op0=ALU.add, op1=ALU.mult),
                              r=[pbr[bp2[ti]], r_pffn, r_sg[si]], w=[r_gv[s_ - k0]])

            items = []
            for s in range(k0, k1):
                si = rot("sg", 2)
                items += [(s, 0, si), (s, 1, si)]
            items = [it + ((("ffn", items[i + 1][0] + items[i + 1][1] * NKF),) if i + 1 < len(items) else ()) for i, it in enumerate(items)]
            pipelined(items, ffn_s1, ffn_diag, ffn_s2)
            if pi == 0 and k0 == 0:
                dbg_dump("gv", gv[:].rearrange("p k t -> p (k t)"), [128, KH * TT], r_gv)
            for ct in range(4):
                for ka in range(k0, k1, 4):
                    wv, rw, nk = w_next("tm")
                    for kk in range(nk):
                        k = ka + kk
                        for (c, rows, col0) in slots_out:
                            fw.op("pe", lambda h, c=c, k=k, kk=kk, rows=rows, col0=col0: h.matmul(pb[c][0:rows, :], lhsT=gv[:, k - k0, col0:col0 + rows], rhs=wv[:, kk, :],
                                                                                               start=(k == k0), stop=(k == k1 - 1)),
                                  r=[r_gv[k - k0], rw], w=[pbr[c]], sig=(kk == nk - 1 and c == slots_out[-1][0]))
                    w_done()
                for (c, rows, col0) in slots_out:
                    fw.op("dve", lambda h, c=c, rows=rows: h.tensor_tensor(out=x1[0:rows, c, ct * 512:(ct + 1) * 512], in0=pb[c][0:rows, :],
                                                                          in1=x1[0:rows, c, ct * 512:(ct + 1) * 512], op=ALU.add),
                          r=[pbr[c], r_x1[c]], w=[r_x1[c]])
        fw.dma("sp", wfin_bc, norm_fin.partition_broadcast(128), w=r_h2T + [r_wfin], stream="wfin")
        for (c, rows, col0) in slots_out:
            stat, r_stat = stat_set()
            fw.op("act", lambda h, c=c, rows=rows, stat=stat: h.activation(out=junk[0:rows, :], in_=x1[0:rows, c, :], func=AF.Square, accum_out=stat[0:rows, 4:5]),
                  r=[r_x1[c]], w=[r_junk, r_stat])
            fw.op("act", lambda h, rows=rows, stat=stat: h.activation(out=stat[0:rows, 5:6], in_=stat[0:rows, 4:5], func=AF.Sqrt, scale=1.0 / D, bias=epsb[0:rows, 0:1]),
                  r=[r_stat, r_eps], w=[r_stat])
            fw.op("dve", lambda h, rows=rows, stat=stat: h.reciprocal(out=stat[0:rows, 6:7], in_=stat[0:rows, 5:6]), r=[r_stat], w=[r_stat])
            fw.op("dve", lambda h, c=c, rows=rows, stat=stat: h.scalar_tensor_tensor(out=x1[0:rows, c, :], in0=x1[0:rows, c, :], scalar=stat[0:rows, 6:7], in1=wfin_bc[0:rows, :],
                                                                         op0=ALU.mult, op1=ALU.mult),
                  r=[r_x1[c], r_stat, r_wfin], w=[r_x1[c]])
            dst_d = y_p[yrows[c] * 128:(yrows[c] + 1) * 128, :] if c < nch else y_s
            fw.dma("sp", dst_d, x1[0:rows, c, :], r=[r_x1[c]])

    fw.barrier()

    stgE = [arena[:, i * 1024:(i + 1) * 1024].bitcast(F32) for i in range(8)]
    r_stgE = [Res("stgE%d" % i) for i in range(8)]
    ecnt = [0]

    def store_fm_rows(src_fm, R, nstrips, dst2d, r_src):
        for s0 in range(0, nstrips, 4):
            ns = min(4, nstrips - s0)
            e = ecnt[0] % 8
            ecnt[0] += 1
            for j in range(ns):
                fw.op("pe", lambda h, j=j: h.transpose(out=pb[e][0:R, j * 128:(j + 1) * 128], in_=src_fm[:, s0 + j, 0:R], identity=ident_f[:]),
                      r=[r_src, r_ident], w=[pbr[e]], sig=(j == ns - 1))
            fw.op("dve" if e % 2 == 0 else "act",
                  (lambda h: h.tensor_copy(out=stgE[e][0:R, 0:ns * 128], in_=pb[e][0:R, 0:ns * 128])) if e % 2 == 0 else
                  (lambda h: h.copy(out=stgE[e][0:R, 0:ns * 128], in_=pb[e][0:R, 0:ns * 128])),
                  r=[pbr[e]], w=[r_stgE[e]])
            fw.dma("sp", dst2d[:, s0 * 128:(s0 + ns) * 128], stgE[e][0:R, 0:ns * 128], r=[r_stgE[e]])

    store_fm_rows(st_pssdc, 3, NSX, o_pssdc, r_st_pssdc)
    store_fm_rows(st_pcfc, 30, NKD, o_pcfc, r_st_pcfc)
    store_fm_rows(st_pffc, 2, 2 * NKF, o_pffc, r_st_pffc)
    for g in range(NG):
        e = ecnt[0] % 8
        ecnt[0] += 1
        for j in range(4):
            fw.op("pe", lambda h, g=g, j=j: h.transpose(out=pb[e][:, j * 128:(j + 1) * 128], in_=hst[:, g, j * 128:(j + 1) * 128], identity=ident_f[:]),
                  r=[r_hst[g], r_ident], w=[pbr[e]], sig=(j == 3))
        fw.op("dve", lambda h: h.tensor_copy(out=stgE[e][:, :], in_=pb[e][:, :]), r=[pbr[e]], w=[r_stgE[e]])
        fw.dma("sp", o_pssm[g * 512:(g + 1) * 512, :].rearrange("(j q) n -> q j n", q=128), stgE[e][:, :].rearrange("q (j n) -> q j n", j=4), r=[r_stgE[e]])

    fw.wait_all("sp")
    nc.sync.drain() if False else None
    return nc, dbg_out


def make_consts():
    ident = np.eye(128, dtype=np.float32)
    U = np.triu(np.ones((128, 128), dtype=np.float32))
    idx = np.arange(128)
    same = ((idx[:, None] // 4) == (idx[None, :] // 4)) & (idx[:, None] < TS) & (idx[None, :] < TS)
    smp = np.zeros((128, 512), dtype=np.float32)
    smp[:, 0:128] = (same & (idx[:, None] <= idx[None, :])).astype(np.float32)
    smp[:, 128:256] = same.astype(np.float32)
    smp[:, 256:320] = 1.0
    smp[:, 448:512] = 1.0
    selbc = ((np.arange(TS)[None, :] // 4) == np.arange(NSEQ_S)[:, None]).astype(np.float32).reshape(-1)
    return ident, U, smp, selbc


def _core_inputs(inputs, core, consts):
    ident, U, smp, selbc = consts
    f = lambda a: np.ascontiguousarray(np.asarray(a, dtype=np.float32))
    s0 = core * NSEQ_S
    b, half = core // 2, core % 2
    xfull = f(inputs["x_prompt"][b])
    if half == 0:
        xp_c = np.concatenate([np.zeros((128, D), np.float32), xfull[0:1024]], axis=0)
        xpre_c = np.zeros((896, D), np.float32)
    else:
        xp_c = xfull[896:2048]
        xpre_c = xfull[0:896]
    m = {
        "xp": np.ascontiguousarray(xp_c), "xpre": np.ascontiguousarray(xpre_c),
        "c_flag": np.full((128,), float(half), np.float32),
        "xs": f(inputs["x_sample"][s0:s0 + NSEQ_S]).reshape(TS, D),
        "w_in": f(inputs["w_in"][0]), "w_out": f(inputs["w_out"][0]),
        "w_up": f(inputs["w_up"][0]), "w_down": f(inputs["w_down"][0]),
        "prm_norm": f(np.stack([inputs["norm_mix_w"][0], inputs["norm_ffn_w"][0], inputs["ssd_norm_w"][0]])),
        "prm_ssd": f(np.concatenate([inputs["ssd_conv_w"][0], inputs["ssd_conv_b"]], axis=0)),
        "prm_cf": f(np.concatenate([inputs["cf_conv_w"][0], inputs["cf_conv_b"], inputs["cf_ln_w"], inputs["cf_ln_b"]], axis=0)),
        "prm_ffn": f(np.concatenate([inputs["ffn_conv_w"][0], inputs["ffn_conv_b"]], axis=0)),
        "prm_head": f(np.concatenate([inputs["ssd_dt_bias"], inputs["ssd_a_log"], inputs["ssd_d"]], axis=0)),
        "norm_fin": f(inputs["norm_final_w"]),
        "c_ident": ident, "c_U": U, "c_smp": smp, "c_selbc": selbc,
        "st_ssm": f(inputs["state_ssm"][0, s0:s0 + NSEQ_S]).reshape(NSEQ_S, D, 128),
        "st_ssdc": f(inputs["state_ssd_conv"][0, s0:s0 + NSEQ_S]).reshape(NSEQ_S * 3, XBC),
        "st_cfc": f(inputs["state_cf_conv"][0, s0:s0 + NSEQ_S]).reshape(NSEQ_S * 30, D),
        "st_ffc": f(inputs["state_ffn_conv"][0, s0:s0 + NSEQ_S]).reshape(NSEQ_S * 2, 2 * FFN),
    }
    return m


_PASSES = [dict(mode="prefix", chunks=[0, 1, 2, 3]), dict(mode="prefix", chunks=[4, 5, 6]),
           dict(chunks=[0, 1, 2, 3, 4], halo=True, yrows=[None, 0, 1, 2, 3]),
           dict(chunks=[5, 6, 7, 8], yrows=[4, 5, 6, 7], sample=True, last=True)]


def kernel(**inputs):
    n = 8
    nc, _ = build_program(_PASSES, with_sample=True)
    consts = make_consts()
    in_maps = [_core_inputs(inputs, c, consts) for c in range(n)]
    res = run_bass_kernel_spmd(nc, in_maps, core_ids=list(range(n)))
    r = res.results
    f = lambda a: np.ascontiguousarray(np.asarray(a, dtype=np.float32))
    y_prompt = np.stack([np.concatenate([f(r[2 * b]["y_p"]), f(r[2 * b + 1]["y_p"])], axis=0) for b in range(4)])
    y_sample = np.concatenate([f(r[c]["y_s"]).reshape(NSEQ_S, 4, D) for c in range(n)], axis=0)
    p_ssm = np.stack([f(r[2 * b + 1]["o_pssm"]).reshape(H, HP, 128) for b in range(4)])[None]
    p_ssdc = np.stack([f(r[2 * b + 1]["o_pssdc"]) for b in range(4)])[None]
    p_cfc = np.stack([f(r[2 * b + 1]["o_pcfc"]) for b in range(4)])[None]
    p_ffc = np.stack([f(r[2 * b + 1]["o_pffc"]) for b in range(4)])[None]
    s_ssm = np.concatenate([f(r[c]["o_sssm"]).reshape(NSEQ_S, H, HP, 128) for c in range(n)], axis=0)[None]
    s_ssdc = np.concatenate([f(r[c]["o_sssdc"]).reshape(NSEQ_S, 3, XBC) for c in range(n)], axis=0)[None]
    s_cfc = np.concatenate([f(r[c]["o_scfc"]).reshape(NSEQ_S, 30, D) for c in range(n)], axis=0)[None]
    s_ffc = np.concatenate([f(r[c]["o_sffc"]).reshape(NSEQ_S, 2, 2 * FFN) for c in range(n)], axis=0)[None]
    return (y_prompt, y_sample, p_ssm, p_ssdc, p_cfc, p_ffc, s_ssm, s_ssdc, s_cfc, s_ffc)
```

```python
import numpy as np
import concourse.bass as bass
import concourse.mybir as mybir
from concourse.bass_utils import run_bass_kernel_spmd

F32 = mybir.dt.float32
BF16 = mybir.dt.bfloat16
AF = mybir.ActivationFunctionType
ALU = mybir.AluOpType

D = 2048
NKD = 16
H = 32
HP = 64
NG = 4
XBC = 3072
NSX = 24
INP = 9248
FFN = 5504
NKF = 43
EPS = 1e-5
SEQ = 2048
NSEQ_S = 16
TS = 64
C_Z, C_XBC, C_DT, C_CFA, C_CFG = 0, 2048, 5120, 5152, 7200


class Res:
    __slots__ = ("name", "w", "r", "excl", "strict")

    def __init__(self, name, excl=False, strict=False):
        self.name = name
        self.w = None
        self.r = {}
        self.excl = excl
        self.strict = strict


class FW:
    def __init__(self, nc):
        self.nc = nc
        self.E = {}
        for name, h in [("pe", nc.tensor), ("act", nc.scalar), ("dve", nc.vector),
                        ("pool", nc.gpsimd), ("sp", nc.sync)]:
            sem = nc.semaphore("c_" + name).__enter__()
            self.E[name] = dict(h=h, sem=sem, cnt=0, waited={}, pr=[], pw=[])
        self.streams = {}
        self.nwait = 0

    def stream(self, name):
        if name not in self.streams:
            self.streams[name] = dict(sem=self.nc.semaphore("d_" + name).__enter__(), n=0)
        return self.streams[name]

    def _deps(self, eng, reads, writes):
        deps = {}

        def add(ev, raw, strict=False):
            if ev is None:
                return
            sem, val, src = ev
            if src == eng:
                if eng == "pe" or eng == "sp":
                    return
                if not raw and not strict:
                    return
            k = id(sem)
            if k not in deps or deps[k][1] < val:
                deps[k] = (sem, val)

        for r in reads:
            add(r.w, True)
            if r.excl:
                for ev in r.r.values():
                    add(ev, False)
        for w in writes:
            add(w.w, w.excl, w.strict)
            for ev in w.r.values():
                add(ev, False, w.strict)
        return deps

    def _wait(self, eng, deps):
        E = self.E[eng]
        for k, (sem, val) in deps.items():
            if E["waited"].get(k, 0) < val:
                E["h"].wait_ge(sem, val)
                E["waited"][k] = val
                self.nwait += 1

    def _commit(self, ev, reads, writes):
        k = id(ev[0])
        for r in reads:
            if r.excl:
                r.w = ev
                r.r = {}
            else:
                r.r[k] = ev
        for w in writes:
            w.w = ev
            w.r = {}

    def op(self, eng, fn, r=(), w=(), sig=True):
        E = self.E[eng]
        self._wait(eng, self._deps(eng, r, w))
        inst = fn(E["h"])
        if sig:
            E["cnt"] += 1
            inst.then_inc(E["sem"], 1)
            ev = (E["sem"], E["cnt"], eng)
            self._commit(ev, list(r) + E["pr"], list(w) + E["pw"])
            E["pr"], E["pw"] = [], []
            return ev
        E["pr"] += list(r)
        E["pw"] += list(w)
        return None

    def dma(self, q, out, in_, r=(), w=(), stream=None):
        E = self.E[q]
        if stream is None:
            rs = list(w) + list(r)
            assert len(rs) == 1, "dma with several resources needs an explicit private stream"
            stream = rs[0].name
        st = self.stream(stream)
        self._wait(q, self._deps("dma", r, w))
        E["h"].dma_start(out=out, in_=in_).then_inc(st["sem"], 16)
        st["n"] += 16
        ev = (st["sem"], st["n"], "dma:" + stream)
        self._commit(ev, r, w)
        return ev

    def barrier(self, engs=("pe", "act", "dve", "sp")):
        for e in engs:
            self.wait_all(e)

    def wait_all(self, eng):
        deps = {}
        for n, E in self.E.items():
            if E["cnt"] > 0 and n != eng:
                deps[id(E["sem"])] = (E["sem"], E["cnt"])
        for st in self.streams.values():
            if st["n"] > 0:
                deps[id(st["sem"])] = (st["sem"], st["n"])
        self._wait(eng, deps)


def build_program(passes, with_sample=True, dbg=None):
    nc = bass.Bass("TRN2", target_bir_lowering=False)
    fw = FW(nc)
    dbg = dbg or []

    def dram_in(name, shape):
        return nc.dram_tensor(name, list(shape), F32, kind="ExternalInput").ap()

    def dram_out(name, shape):
        return nc.dram_tensor(name, list(shape), F32, kind="ExternalOutput").ap()

    def _has_s(p):
        return bool(p.get("sample")) and with_sample
    NSLOT = max(len(p["chunks"]) + (1 if _has_s(p) else 0) for p in passes)
    TT = max(len(p["chunks"]) * 128 + (TS if _has_s(p) else 0) for p in passes)
    PRE_N = 30 + max(len(p["chunks"]) * 128 + (NSEQ_S * 34 if _has_s(p) else 0) for p in passes)
    NCH_IN = max(max(p["chunks"]) for p in passes if p.get("mode") != "prefix") + 1
    NCH_PRE = max([max(p["chunks"]) + 1 for p in passes if p.get("mode") == "prefix"] + [1])
    NCH_OUT = max(max([y for y in p.get("yrows", p["chunks"]) if y is not None] + [0]) for p in passes if p.get("mode") != "prefix") + 1
    xp = dram_in("xp", [NCH_IN * 128, D])
    xpre = dram_in("xpre", [NCH_PRE * 128, D])
    c_flag = dram_in("c_flag", [128])
    xsd = dram_in("xs", [TS, D])
    w_in = dram_in("w_in", [D, INP])
    w_out = dram_in("w_out", [2 * D, D])
    w_up = dram_in("w_up", [D, 2 * FFN])
    w_down = dram_in("w_down", [FFN, D])
    prm_norm = dram_in("prm_norm", [3, D])
    prm_ssd = dram_in("prm_ssd", [5, XBC])
    prm_cf = dram_in("prm_cf", [34, D])
    prm_ffn = dram_in("prm_ffn", [4, 2 * FFN])
    prm_head = dram_in("prm_head", [3, H])
    norm_fin = dram_in("norm_fin", [D])
    c_ident = dram_in("c_ident", [128, 128])
    c_U = dram_in("c_U", [128, 128])
    c_smp = dram_in("c_smp", [128, 512])
    c_selbc = dram_in("c_selbc", [NSEQ_S * TS])
    st_ssm = dram_in("st_ssm", [NSEQ_S, D, 128])
    st_ssdc = dram_in("st_ssdc", [NSEQ_S * 3, XBC])
    st_cfc = dram_in("st_cfc", [NSEQ_S * 30, D])
    st_ffc = dram_in("st_ffc", [NSEQ_S * 2, 2 * FFN])

    y_p = dram_out("y_p", [NCH_OUT * 128, D])
    y_s = dram_out("y_s", [TS, D])
    o_pssm = dram_out("o_pssm", [D, 128])
    o_pssdc = dram_out("o_pssdc", [3, XBC])
    o_pcfc = dram_out("o_pcfc", [30, D])
    o_pffc = dram_out("o_pffc", [2, 2 * FFN])
    o_sssm = dram_out("o_sssm", [NSEQ_S, D, 128])
    o_sssdc = dram_out("o_sssdc", [NSEQ_S * 3, XBC])
    o_scfc = dram_out("o_scfc", [NSEQ_S * 30, D])
    o_sffc = dram_out("o_sffc", [NSEQ_S * 2, 2 * FFN])
    dbg_out = {}

    def sb(name, shape, dt=F32):
        return nc.sbuf_tensor(name, list(shape), dt).__enter__()

    pb = [nc.psum_tensor("pb%d" % i, [128, 512], F32).__enter__() for i in range(8)]
    pbr = [Res("pb%d" % i, excl=True) for i in range(8)]

    def pbf(i):
        return pb[i][:].bitcast(BF16)


    n_r1 = NSLOT * D * 2
    n_r2 = NKD * TT
    KH = 22
    n_r3 = NSX * TT
    arena = sb("arena", [128, n_r1 + n_r2 + n_r3], BF16)
    o1, o2, o3 = 0, n_r1, n_r1 + n_r2
    x1 = arena[:, o1:o1 + n_r1].bitcast(F32).rearrange("p (c f) -> p c f", c=NSLOT)
    hT = arena[:, o1:o1 + NKD * TT].rearrange("p (k t) -> p k t", k=NKD)
    zs = arena[:, o1 + NKD * TT:o1 + NKD * TT + NSLOT * D].rearrange("p (c f) -> p c f", c=NSLOT)
    assert NKD * TT + NSLOT * D <= n_r1
    mixcf = arena[:, o2:o2 + n_r2].rearrange("p (k t) -> p k t", k=NKD)
    h2T = mixcf
    gv = arena[:, o3:o3 + KH * TT].rearrange("p (k t) -> p k t", k=KH)
    xc = arena[:, o3:o3 + NSX * TT].rearrange("p (k t) -> p k t", k=NSX)

    r_x1 = [Res("x1_%d" % c) for c in range(NSLOT)]
    r_hT = [Res("hT_%d" % c) for c in range(NSLOT)]
    r_h2T = [Res("h2T_%d" % c) for c in range(NSLOT)]
    r_zs = [Res("zs_%d" % c) for c in range(NSLOT)]
    r_xc = [Res("xc_%d" % s) for s in range(NSX)]
    r_mixcf = [Res("mixcf_%d" % s) for s in range(NKD)]
    r_gv = [Res("gv_%d" % s) for s in range(KH)]

    ident_f = sb("ident_f", [128, 128]); r_ident = Res("ident")
    ident_b = sb("ident_b", [128, 128], BF16)
    U_f = sb("U_f", [128, 128]); r_U = Res("U")
    ones_f = sb("ones_f", [128, 128]); ones_b = sb("ones_b", [128, 128], BF16); r_ones = Res("ones")
    epsb = sb("epsb", [128, 1]); r_eps = Res("eps")
    pn_fm = sb("pn_fm", [128, NKD, 3]); r_pn = Res("pn")
    pssd_fm = sb("pssd_fm", [128, NSX, 5]); r_pssd = Res("pssd")
    pcf_fm = sb("pcf_fm", [128, NKD, 34]); r_pcf = Res("pcf")
    pffn_fm = sb("pffn_fm", [128, 2 * NKF, 4]); r_pffn = Res("pffn")
    head_bc = sb("head_bc", [128, 3, H]); r_head = Res("head")
    r_wfin = Res("wfin")
    wdt = sb("wdt", [128, NKD, 32], BF16); r_wdt = Res("wdt")
    stg = sb("stg", [128, 512]); r_stg = Res("stg")
    flg = sb("flg", [128, 1]); r_flg = Res("flg")
    fw.dma("sp", flg[:], c_flag.rearrange("(p o) -> p o", o=1), w=[r_flg])

    fw.dma("sp", ident_f[:], c_ident, w=[r_ident])
    fw.dma("sp", U_f[:], c_U, w=[r_U])
    fw.dma("sp", head_bc[:].rearrange("p a h -> p (a h)"), prm_head.rearrange("a h -> (a h)").partition_broadcast(128), w=[r_head])
    fw.dma("pool", wdt[:], w_in[:, C_DT:C_DT + 32].rearrange("(k p) c -> p k c", p=128), w=[r_wdt])
    fw.op("dve", lambda h: h.tensor_copy(out=ident_b[:], in_=ident_f[:]), r=[r_ident], w=[r_ident])
    fw.op("dve", lambda h: h.memset(ones_f[:], 1.0), w=[r_ones])
    fw.op("dve", lambda h: h.memset(ones_b[:], 1.0), w=[r_ones])
    fw.op("dve", lambda h: h.memset(epsb[:], EPS), w=[r_eps])
    fw.op("act", lambda h: h.activation(out=head_bc[:, 1, :], in_=head_bc[:, 1, :], func=AF.Exp), r=[r_head], w=[r_head])
    fw.op("dve", lambda h: h.tensor_scalar(out=head_bc[:, 1, :], in0=head_bc[:, 1, :], scalar1=-1.0, scalar2=None, op0=ALU.mult), r=[r_head], w=[r_head])

    def load_rows_fm(src2d, R, C, dst_fm, r_dst, bank):
        for c0 in range(0, C, 512):
            cw = min(512, C - c0)
            ns = cw // 128
            fw.dma("sp", stg[0:R, 0:cw], src2d[:, c0:c0 + cw], w=[r_stg])
            for j in range(ns):
                fw.op("pe", lambda h, j=j: h.transpose(out=pb[bank][:, j * R:(j + 1) * R], in_=stg[0:R, j * 128:(j + 1) * 128],
                                                         identity=ident_f[0:R, 0:R]),
                      r=[r_stg, r_ident], w=[pbr[bank]], sig=(j == ns - 1))
            fw.op("dve", lambda h: h.tensor_copy(out=dst_fm[:, c0 // 128:c0 // 128 + ns, :],
                                                 in_=pb[bank][:, 0:ns * R].rearrange("p (s r) -> p s r", s=ns)),
                  r=[pbr[bank]], w=[r_dst])

    load_rows_fm(prm_norm, 3, D, pn_fm, r_pn, 0)
    load_rows_fm(prm_ssd, 5, XBC, pssd_fm, r_pssd, 1)
    load_rows_fm(prm_cf, 34, D, pcf_fm, r_pcf, 2)
    load_rows_fm(prm_ffn, 4, 2 * FFN, pffn_fm, r_pffn, 3)

    if with_sample:
        smp_f = sb("smp_f", [128, 512]); r_smp = Res("smp")
        Us_f, Bd_f, hA_f, hB_f = smp_f[:, 0:128], smp_f[:, 128:256], smp_f[:, 256:384], smp_f[:, 384:512]
        sel_bc = sb("sel_bc", [128, NSEQ_S, TS], BF16); r_selbc = Res("selbc")
        fw.dma("sp", smp_f[:], c_smp, w=[r_smp])
        fw.dma("pool", sel_bc[:].rearrange("p a b -> p (a b)"), c_selbc.partition_broadcast(128), w=[r_selbc])
        so_t = [sb("so_t%d" % i, [128, 64]) for i in range(2)]; r_so = [Res("so_t%d" % i) for i in range(2)]
        stgo = [sb("stgo%d" % i, [64, 128]) for i in range(2)]; r_stgo = [Res("stgo%d" % i) for i in range(2)]
        stg2 = [stg, sb("stg2_1", [128, 512])]; r_stg2 = [r_stg, Res("stg2_1")]
        spf = {}
        hbuf = sb("hbuf", [128, 4, 2048], BF16)
        h0b = [hbuf[:, i, :].rearrange("p (b n) -> p b n", b=16) for i in range(2)]
        h0T = [hbuf[:, 2 + i, :] for i in range(2)]
        h0f = [hbuf[:, 2 * i:2 * i + 2, :].rearrange("p a f -> p (a f)").bitcast(F32).rearrange("p (b n) -> p b n", b=16) for i in range(2)]
        r_hb = [Res("hb%d" % i) for i in range(4)]
        Cm = [sb("Cm%d" % i, [128, NG, TS], BF16) for i in range(2)]; r_Cm = [Res("Cm%d" % i) for i in range(2)]
        Bm = [sb("Bm%d" % i, [64, 512], BF16) for i in range(2)]; r_Bm = [Res("Bm%d" % i) for i in range(2)]
        dec_fm = sb("dec_fm", [128, NSEQ_S, 16]); r_dec = Res("dec_fm")
        rhs_ab = sb("rhs_ab", [64, 2, NSEQ_S * 16]); r_rab = Res("rhs_ab")
        fw.dma("sp", o_scfc.rearrange("(q r) c -> q r c", r=30)[:, 0:26, :], st_cfc.rearrange("(q r) c -> q r c", r=30)[:, 4:30, :], stream="d2d")

    hst = sb("hst", [128, NG, 512]); r_hst = [Res("hst%d" % g) for g in range(NG)]
    hst_b = sb("hst_b", [128, NG, 512], BF16); r_hstb = [Res("hstb%d" % g) for g in range(NG)]
    halo_ssd = sb("halo_ssd", [128, NSX, 3], BF16); r_halo_ssd = Res("halo_ssd")
    halo_cf = sb("halo_cf", [128, NKD, 30], BF16); r_halo_cf = Res("halo_cf")
    halo_ffn = sb("halo_ffn", [128, 2 * NKF, 2], BF16); r_halo_ffn = Res("halo_ffn")
    st_pssdc = sb("st_pssdc", [128, NSX, 3]); r_st_pssdc = Res("st_pssdc")
    st_pcfc = sb("st_pcfc", [128, NKD, 30]); r_st_pcfc = Res("st_pcfc")
    st_pffc = sb("st_pffc", [128, 2 * NKF, 2]); r_st_pffc = Res("st_pffc")
    for t, rr in [(hst, r_hst), (hst_b, r_hstb)]:
        fw.op("dve", lambda h, t=t: h.memset(t[:].rearrange("p g f -> p (g f)"), 0.0), w=rr)
    for t, rr in [(halo_ssd, r_halo_ssd), (halo_cf, r_halo_cf), (halo_ffn, r_halo_ffn)]:
        fw.op("dve", lambda h, t=t: h.memset(t[:].rearrange("p s f -> p (s f)"), 0.0), w=[rr])

    NSL = 5
    wsl = [sb("wsl%d" % i, [128, 2048], BF16) for i in range(NSL)]
    r_wsl = [Res("wsl%d" % i) for i in range(NSL)]
    units = []

    def add_fm_unit(W, k0, nk, c0):
        units.append(("fm", W[k0 * 128:(k0 + nk) * 128, c0:c0 + 128].rearrange("(k p) c -> p k c", p=128), nk))

    def add_tm_unit(W, k0, nk, c0):
        units.append(("tm", W[k0 * 128:(k0 + nk) * 128, c0:c0 + 512].rearrange("(k p) c -> p k c", p=128), nk))

    for p in passes:
        if p.get("mode") == "prefix":
            for s in range(20):
                add_fm_unit(w_in, 0, 16, C_XBC + 128 * s)
            continue
        for ct in range(4):
            for u in range(4):
                add_tm_unit(w_in, 4 * u, 4, C_Z + 512 * ct)
        for s in range(NSX):
            add_fm_unit(w_in, 0, 16, C_XBC + 128 * s)
        for s in range(NKD):
            add_fm_unit(w_in, 0, 16, C_CFG + 128 * s)
            add_fm_unit(w_in, 0, 16, C_CFA + 128 * s)
        for ct in range(4):
            for u in range(8):
                add_tm_unit(w_out, 4 * u, 4, 512 * ct)
        for (k0, k1) in ((0, 22), (22, NKF)):
            for s in range(k0, k1):
                add_fm_unit(w_up, 0, 16, 128 * s)
                add_fm_unit(w_up, 0, 16, FFN + 128 * s)
            for ct in range(4):
                for ka in range(k0, k1, 4):
                    add_tm_unit(w_down, ka, min(4, k1 - ka), 512 * ct)
    wq = dict(issued=0, used=0)

    def w_prefetch():
        while wq["issued"] < len(units) and wq["issued"] < wq["used"] + NSL:
            i = wq["issued"]
            kind, ap, nk = units[i]
            sl = i % NSL
            if kind == "fm":
                dst = wsl[sl][:, 0:nk * 128].rearrange("p (k c) -> p k c", k=nk)
            else:
                dst = wsl[sl][:, 0:nk * 512].rearrange("p (k c) -> p k c", k=nk)
            fw.dma("pool", dst, ap, w=[r_wsl[sl]], stream="w%d" % sl)
            wq["issued"] += 1

    w_prefetch()

    def w_next(kind):
        w_prefetch()
        i = wq["used"]
        k, ap, nk = units[i]
        assert k == kind, (k, kind, i)
        sl = i % NSL
        cols = 128 if kind == "fm" else 512
        return wsl[sl][:, 0:nk * cols].rearrange("p (k c) -> p k c", k=nk), r_wsl[sl], nk

    def w_done():
        wq["used"] += 1
        w_prefetch()

    xt = [arena[:, o3 + i * 2 * D:o3 + (i + 1) * 2 * D].bitcast(F32) for i in range(2)]; r_xt = [Res("xt%d" % i) for i in range(2)]
    wfin_bc = arena[:, o2:o2 + 2 * D].bitcast(F32)
    xn0 = sb("xn", [128, D], BF16); r_xn0 = Res("xn")
    xn = xn0; r_xn = r_xn0
    junk = xn0; r_junk = r_xn0
    junkA = arena[:, o2:o2 + D]; r_junkA = Res("junkA")
    xnA = arena[:, o2 + D:o2 + 2 * D]; r_xnA = Res("xnA")
    junkD = arena[:, o3:o3 + D]; r_junkD = Res("junkD")
    xnD = arena[:, o3 + D:o3 + 2 * D]; r_xnD = Res("xnD")
    stat_all = sb("stat", [128, 64]); r_stats = [Res("stat%d" % i) for i in range(8)]
    stat_n = [0]

    def stat_set():
        i = stat_n[0] % 8
        stat_n[0] += 1
        return stat_all[:, i * 8:(i + 1) * 8], r_stats[i]
    dt_tok = sb("dt_tok", [128, NSLOT, H]); r_dt = [Res("dt%d" % c) for c in range(NSLOT)]
    pre = [sb("pre%d" % i, [128, PRE_N], BF16) for i in range(2)]
    r_pre = [Res("pre%d" % i) for i in range(2)]
    sg = [sb("sg%d" % i, [128, TT]) for i in range(2)]; r_sg = [Res("sg%d" % i) for i in range(2)]
    dgR = sb("dgR", [128, 4096], BF16)
    diag = [dgR[:, 0:31 * 128].rearrange("p (i c) -> p i c", i=31), sb("diag1", [128, 4, 128], BF16), sb("diag2", [128, 4, 128], BF16)]
    r_diag = [Res("diag%d" % i) for i in range(3)]
    cnt = dict(pre=0, sg=0, diag=0, xt=0, bank=0, stg2=0, so=0, bankcf=0)

    def rot(name, n):
        i = cnt[name] % n
        cnt[name] += 1
        return i

    lnst = sb("lnst", [128, 2, max(TT, 512)]); r_lnst = Res("lnst")
    sm = sb("sm", [128, 5, H]); r_sm = Res("sm")
    dAh = sb("dAh", [128, H], BF16)
    Rg = dgR[:, 0:2048].bitcast(F32); r_Rg = Res("Rg")
    ecs = dgR[:, 2048:4096].bitcast(F32); r_ecs = Res("ecs")
    seg = lnst[:].rearrange("p a t -> p (a t)")[:, 0:1024]; r_seg = r_lnst
    mtcs = sb("mtcs", [128, 4096], BF16)
    MT = [mtcs[:, 0:1024], mtcs[:, 2048:3072]]; r_MT = [Res("MT%d" % i) for i in range(2)]
    Cs = [mtcs[:, 1024:2048], mtcs[:, 3072:4096]]; r_Cs = [Res("Cs%d" % i) for i in range(2)]
    CBTm = [sb("CBTm%d" % i, [128, 128]) for i in range(2)]; r_CBTm = [Res("CBTm%d" % i) for i in range(2)]
    xs_tok = arena[:, o1:o1 + D]; r_xs_tok = Res("xs_tok")
    xdt_tok = arena[:, o1 + D:o1 + 2 * D]; r_xdt = Res("xdt")
    xw_tok = xdt_tok; r_xw = r_xdt
    t1 = arena[:, o1 + 2 * D:o1 + 4 * D].bitcast(F32); r_t1 = Res("t1")
    assert 4 * D <= NKD * TT
    xsD = sb("xsD", [128, 512]); r_xsD = Res("xsD", strict=True)
    B_tok = sb("B_tok", [128, 512], BF16); r_B_tok = Res("B_tok")
    r_xc_ssd = Res("xc_ssd_dummy")

    def norm_to_T(src, rows, r_src, wcol, dstT, r_dst, col0, banks, jnk=None, xnb=None):
        stat, r_stat = stat_set()
        jk, r_jk = jnk if jnk is not None else (junk, r_junk)
        xn, r_xn = xnb if xnb is not None else (xn0, r_xn0)
        fw.op("act", lambda h: h.activation(out=jk[0:rows, :], in_=src, func=AF.Square, accum_out=stat[0:rows, 0:1]),
              r=[r_src], w=[r_jk, r_stat])
        fw.op("act", lambda h: h.activation(out=stat[0:rows, 1:2], in_=stat[0:rows, 0:1], func=AF.Sqrt, scale=1.0 / D, bias=epsb[0:rows, 0:1]),
              r=[r_stat, r_eps], w=[r_stat])
        fw.op("dve", lambda h: h.reciprocal(out=stat[0:rows, 2:3], in_=stat[0:rows, 1:2]), r=[r_stat], w=[r_stat])
        fw.op("dve", lambda h: h.tensor_scalar(out=xn[0:rows, :], in0=src, scalar1=stat[0:rows, 2:3], scalar2=None, op0=ALU.mult),
              r=[r_src, r_stat], w=[r_xn])
        for half in range(2):
            b = banks[half]
            for j in range(8):
                k = half * 8 + j
                fw.op("pe", lambda h, j=j, k=k, b=b: h.transpose(out=pbf(b)[:, j * 128:j * 128 + rows], in_=xn[0:rows, k * 128:(k + 1) * 128],
                                                                   identity=ident_b[0:rows, 0:rows]),
                      r=[r_xn, r_ident], w=[pbr[b]], sig=(j == 7))
            fw.op("dve", lambda h, half=half, b=b: h.tensor_tensor(
                out=dstT[:, half * 8:half * 8 + 8, col0:col0 + rows],
                in0=pbf(b).rearrange("p (k t) -> p k t", k=8)[:, :, 0:rows],
                in1=pn_fm[:, half * 8:half * 8 + 8, wcol:wcol + 1].to_broadcast([128, 8, rows]), op=ALU.mult),
                r=[pbr[b], r_pn], w=r_dst)

    def build_diag(dg, prm_fm, s, ntap, r_prm, r_dg, eng="dve"):
        for i in range(ntap):
            if eng == "dve":
                fw.op("dve", lambda h, i=i: h.tensor_scalar(out=dg[:, i, :], in0=ident_f[:], scalar1=prm_fm[:, s, i:i + 1], scalar2=None, op0=ALU.mult),
                      r=[r_ident, r_prm], w=[r_dg], sig=(i == ntap - 1))
            else:
                fw.op("act", lambda h, i=i: h.activation(out=dg[:, i, :], in_=ident_f[:], func=AF.Identity, scale=prm_fm[:, s, i:i + 1]),
                      r=[r_ident, r_prm], w=[r_dg], sig=(i == ntap - 1))

    def dbg_dump(name, ap, shape, rr):
        if name in dbg:
            dbg_out[name] = nc.dram_tensor("dbg_" + name, list(shape), ap.dtype, kind="ExternalOutput").ap()
            fw.dma("sp", dbg_out[name], ap, r=rr, stream="dbg_" + name)

    for pi, P in enumerate(passes):
        chunks = P["chunks"]
        nch = len(chunks)
        has_s = _has_s(P)
        last = bool(P.get("last"))
        prefix = P.get("mode") == "prefix"
        halo = bool(P.get("halo"))
        yrows = P.get("yrows", chunks)
        xsrc = xpre if prefix else xp
        Tm = nch * 128
        slots = [(c, 128, c * 128) for c in range(nch)]
        if has_s:
            slots.append((nch, TS, Tm))
        slots_out = [sl for sl in slots if not (halo and sl[0] == 0)]
        mtiles = [(c0, min(512, Tm - c0)) for c0 in range(0, Tm, 512)]
        tiles = list(mtiles)
        if has_s:
            tiles.append((Tm, TS))
        assert len(tiles) <= 2
        LT = len(mtiles) - 1
        LN = mtiles[-1][1]

        fw.barrier()
        for (c, rows, col0) in slots:
            xi = rot("xt", 2)
            src_d = xsrc[chunks[c] * 128:(chunks[c] + 1) * 128, :] if c < nch else xsd
            fw.dma("sp", xt[xi][0:rows, :], src_d, w=[r_xt[xi]])
            norm_to_T(xt[xi][0:rows, :], rows, r_xt[xi], 0, hT, [r_hT[c]], col0, (6, 7), jnk=(junkA, r_junkA),
                      xnb=((xnA, r_xnA) if c % 2 else None))
        if pi == 0:
            dbg_dump("hT", hT[:].rearrange("p k t -> p (k t)"), [128, NKD * TT], r_hT)

        for (c, rows, col0) in slots:
            for k in range(NKD):
                fw.op("pe", lambda h, k=k, c=c: h.matmul(pb[5][0:rows, c * 32:(c + 1) * 32], lhsT=hT[:, k, col0:col0 + rows], rhs=wdt[:, k, :],
                                                          start=(k == 0), stop=(k == NKD - 1)),
                      r=[r_hT[c], r_wdt], w=[pbr[5]], sig=(k == NKD - 1))
        for (c, rows, col0) in slots:
            fw.op("dve", lambda h, c=c: h.tensor_tensor(out=dt_tok[0:rows, c, :], in0=pb[5][0:rows, c * 32:(c + 1) * 32], in1=head_bc[0:rows, 0, :], op=ALU.add),
                  r=[pbr[5], r_head], w=[r_dt[c]])
            fw.op("act", lambda h, c=c: h.activation(out=dt_tok[0:rows, c, :], in_=dt_tok[0:rows, c, :], func=AF.Exp), r=[r_dt[c]], w=[r_dt[c]])
            fw.op("act", lambda h, c=c: h.activation(out=dt_tok[0:rows, c, :], in_=dt_tok[0:rows, c, :], func=AF.Ln, bias=1.0), r=[r_dt[c]], w=[r_dt[c]])

        if not prefix:

            for ct in range(4):
                for u in range(4):
                    wv, rw, nk = w_next("tm")
                    for kk in range(nk):
                        k = 4 * u + kk
                        for (c, rows, col0) in slots:
                            fw.op("pe", lambda h, c=c, k=k, kk=kk: h.matmul(pb[c][0:rows, :], lhsT=hT[:, k, col0:col0 + rows], rhs=wv[:, kk, :],
                                                                           start=(k == 0), stop=(k == NKD - 1)),
                                  r=[r_hT[c], rw], w=[pbr[c]], sig=(kk == nk - 1 and c == slots[-1][0]))
                    w_done()
                for (c, rows, col0) in slots:
                    fw.op("act", lambda h, c=c: h.activation(out=zs[0:rows, c, ct * 512:(ct + 1) * 512], in_=pb[c][0:rows, :], func=AF.Silu),
                          r=[pbr[c]], w=[r_zs[c]])

        def fm_gemm(wv, rw, src, r_src, banks):
            for k in range(NKD):
                for ti, (c0, n) in enumerate(tiles):
                    b = banks[ti]
                    fw.op("pe", lambda h, k=k, b=b, c0=c0, n=n: h.matmul(pb[b][:, 0:n], lhsT=wv[:, k, :], rhs=src[:, k, c0:c0 + n],
                                                                        start=(k == 0), stop=(k == NKD - 1)),
                          r=r_src + [rw], w=[pbr[b]], sig=(k == NKD - 1))

        def conv_pe(pr, r_p, dg, r_dg, ntap, hl, banks):
            for ti, (c0, n) in enumerate(tiles):
                b = banks[ti]
                for i in range(ntap):
                    if ti < len(mtiles):
                        rhs = pr[:, c0 + i:c0 + i + n]
                        out = pb[b][:, 0:n]
                    else:
                        base = hl + Tm
                        rhs = pr[:, base:base + NSEQ_S * (hl + 4)].rearrange("p (a b) -> p a b", a=NSEQ_S)[:, :, i:i + 4]
                        out = pb[b][:, 0:TS].rearrange("p (a b) -> p a b", a=NSEQ_S)
                    fw.op("pe", lambda h, i=i, rhs=rhs, out=out: h.matmul(out, lhsT=dg[:, i, :], rhs=rhs, start=(i == 0), stop=(i == ntap - 1)),
                          r=[r_p, r_dg], w=[pbr[b]], sig=(i == ntap - 1))

        SKIND = {"ssd": (st_ssdc, o_sssdc, 3, 1, 3), "cf": (st_cfc, o_scfc, 4, 0, 30), "ffn": (st_ffc, o_sffc, 2, 2, 2)}

        def sample_state_dma(kind, strip):
            state2d, _, _, _, hl = SKIND[kind]
            R_all = NSEQ_S * hl
            ngrp = 4 if R_all > 128 else 1
            rg = R_all // ngrp
            bi = rot("stg2", 2)
            fw.dma("sp", stg2[bi][0:rg, 0:ngrp * 128].rearrange("r (j c) -> r j c", j=ngrp),
                   state2d[:, strip * 128:(strip + 1) * 128].rearrange("(j r) c -> r j c", r=rg), w=[r_stg2[bi]])
            spf[(kind, strip)] = bi

        def sample_in(pr, r_p, kind, strip, gbank, nxt=None, sgb=None, r_sgb=None):
            state2d, out2d, nkk, t0, hl = SKIND[kind]
            R_all = NSEQ_S * hl
            ngrp = 4 if R_all > 128 else 1
            rg = R_all // ngrp
            if (kind, strip) not in spf:
                sample_state_dma(kind, strip)
            bi = spf.pop((kind, strip))
            xbank = 6
            for j in range(ngrp):
                fw.op("pe", lambda h, j=j: h.transpose(out=pb[xbank][:, j * rg:(j + 1) * rg], in_=stg2[bi][0:rg, j * 128:(j + 1) * 128], identity=ident_f[0:rg, 0:rg]),
                      r=[r_stg2[bi], r_ident], w=[pbr[xbank]], sig=(j == ngrp - 1))
            if nxt is not None:
                sample_state_dma(*nxt)
            base = hl + Tm
            pr_s = pr[:, base:base + NSEQ_S * (hl + 4)].rearrange("p (a b) -> p a b", a=NSEQ_S)
            fw.op("act", lambda h: h.copy(out=pr_s[:, :, 0:hl], in_=pb[xbank][:, 0:R_all].rearrange("p (a b) -> p a b", a=NSEQ_S)), r=[pbr[xbank]], w=[r_p])
            g4 = pb[gbank][:, 0:TS].rearrange("p (a b) -> p a b", a=NSEQ_S)
            oi = rot("so", 2)
            so_v = so_t[oi][:, 0:nkk * 16].rearrange("p (t q) -> p q t", t=nkk)
            if kind == "cf":
                sgv = sgb[:, Tm:Tm + TS].rearrange("p (a b) -> p a b", a=NSEQ_S)
                fw.op("dve", lambda h: h.tensor_tensor(out=pr_s[:, :, hl:hl + 4], in0=g4, in1=sgv, op=ALU.mult), r=[pbr[gbank], r_sgb], w=[r_p])
                fw.op("dve", lambda h: h.tensor_tensor(out=so_v, in0=g4, in1=sgv, op=ALU.mult), r=[pbr[gbank], r_sgb], w=[r_so[oi]])
            else:
                fw.op("act", lambda h: h.copy(out=pr_s[:, :, hl:hl + 4], in_=g4), r=[pbr[gbank]], w=[r_p])
                fw.op("dve", lambda h: h.tensor_copy(out=so_v, in_=g4[:, :, t0:4]), r=[pbr[gbank]], w=[r_so[oi]])
            return (kind, strip, oi)

        def sample_out(c):
            kind, strip, oi = c
            state2d, out2d, nkk, t0, hl = SKIND[kind]
            scols = slice(strip * 128, (strip + 1) * 128)
            fw.op("pe", lambda h: h.transpose(out=pb[7][0:nkk * 16, 0:128], in_=so_t[oi][:, 0:nkk * 16], identity=ident_f[:]), r=[r_so[oi], r_ident], w=[pbr[7]])
            fw.op("dve", lambda h: h.tensor_copy(out=stgo[oi][0:nkk * 16, :], in_=pb[7][0:nkk * 16, 0:128]), r=[pbr[7]], w=[r_stgo[oi]])
            o3d = out2d.rearrange("(q r) c -> q r c", r=hl)
            for t in range(nkk):
                fw.dma("sp", o3d[:, hl - nkk + t, scols], stgo[oi][t * 16:(t + 1) * 16, :], r=[r_stgo[oi]])

        def ssd_sample():
            c = nch
            R_ = slice(0, TS)
            scols = slice(Tm, Tm + TS)
            ssd_block(c, TS, Tm, Us_f, True)
            fw.op("act", lambda h: h.activation(out=sm[R_, 3, :], in_=sm[R_, 1, :], func=AF.Exp), r=[r_sm], w=[r_sm])
            fw.op("pe", lambda h: h.matmul(pb[3][R_, 0:H], lhsT=Bd_f[R_, R_], rhs=sm[R_, 0, :], start=True, stop=True), r=[r_smp, r_sm], w=[pbr[3]])
            fw.op("dve", lambda h: h.tensor_tensor(out=sm[R_, 2, :], in0=pb[3][R_, 0:H], in1=sm[R_, 1, :], op=ALU.subtract), r=[pbr[3], r_sm], w=[r_sm])
            fw.op("act", lambda h: h.activation(out=sm[R_, 2, :], in_=sm[R_, 2, :], func=AF.Exp), r=[r_sm], w=[r_sm])
            selv = Bd_f[R_, 0:TS].rearrange("p (q t) -> p q t", t=4)[:, :, 0]
            for ab in range(2):
                dA2 = sm[R_, 0, :].rearrange("p (b two) -> p b two", two=2)[:, :, ab]
                fw.op("dve", lambda h, ab=ab, dA2=dA2: h.tensor_tensor(out=rhs_ab[:, ab, :].rearrange("p (q b) -> p q b", q=NSEQ_S),
                                                                       in0=selv.unsqueeze(2).to_broadcast([TS, NSEQ_S, 16]),
                                                                       in1=dA2.unsqueeze(1).to_broadcast([TS, NSEQ_S, 16]), op=ALU.mult),
                      r=[r_smp, r_sm], w=[r_rab])
            fw.op("pe", lambda h: h.matmul(pb[3][:, 256:512], lhsT=hA_f[R_, :], rhs=rhs_ab[:, 0, :], start=True, stop=False), r=[r_smp, r_rab], w=[pbr[3]], sig=False)
            fw.op("pe", lambda h: h.matmul(pb[3][:, 256:512], lhsT=hB_f[R_, :], rhs=rhs_ab[:, 1, :], start=False, stop=True), r=[r_smp, r_rab], w=[pbr[3]])
            fw.op("act", lambda h: h.activation(out=dec_fm[:].rearrange("p q b -> p (q b)"), in_=pb[3][:, 256:512], func=AF.Exp), r=[pbr[3]], w=[r_dec])
            for q in range(NSEQ_S):
                qi = q % 2
                fw.dma("pool", h0b[qi], st_ssm[q].rearrange("(b m) n -> m b n", m=128), w=[r_hb[qi]])
                for half in range(2):
                    for j in range(8):
                        blk = half * 8 + j
                        fw.op("pe", lambda h, j=j, blk=blk, half=half: h.transpose(out=pbf(4 + half)[:, j * 128:(j + 1) * 128], in_=h0b[qi][:, blk, :], identity=ident_b[:]),
                              r=[r_hb[qi], r_ident], w=[pbr[4 + half]], sig=(j == 7))
                    fw.op("act" if half == 0 else "dve",
                          (lambda h, half=half: h.copy(out=h0T[qi][:, half * 1024:(half + 1) * 1024], in_=pbf(4 + half))) if half == 0 else
                          (lambda h, half=half: h.tensor_copy(out=h0T[qi][:, half * 1024:(half + 1) * 1024], in_=pbf(4 + half))),
                          r=[pbr[4 + half]], w=[r_hb[2 + qi]])
                fw.op("dve", lambda h: h.tensor_tensor(out=Cm[qi][:], in0=xc[:, 20:24, scols], in1=sel_bc[:, q, :].unsqueeze(1).to_broadcast([128, NG, TS]), op=ALU.mult),
                      r=r_xc[20:24] + [r_selbc], w=[r_Cm[qi]])
                for g in range(NG):
                    fw.op("pe", lambda h, g=g: h.matmul(pb[g][R_, :], lhsT=Cm[qi][:, g, :], rhs=h0T[qi][:, g * 512:(g + 1) * 512], start=(q == 0), stop=(q == NSEQ_S - 1)),
                          r=[r_Cm[qi], r_hb[2 + qi]], w=[pbr[g]], sig=(g == NG - 1))
            for g in range(NG):
                fw.op("dve", lambda h, g=g: h.tensor_tensor(out=xsD[R_, :].rearrange("p (h q) -> p h q", h=8), in0=pb[g][R_, :].rearrange("p (h q) -> p h q", h=8),
                                                            in1=sm[R_, 3, g * 8:(g + 1) * 8].unsqueeze(2).to_broadcast([TS, 8, HP]), op=ALU.mult),
                      r=[pbr[g], r_sm], w=[r_xsD])
                fw.op("dve", lambda h, g=g: h.tensor_tensor(out=t1[R_, g * 512:(g + 1) * 512], in0=t1[R_, g * 512:(g + 1) * 512], in1=xsD[R_, :], op=ALU.add),
                      r=[r_xsD, r_t1], w=[r_t1])
            fw.op("dve", lambda h: h.tensor_tensor(out=xw_tok[R_, :].rearrange("p (h q) -> p h q", h=H), in0=xdt_tok[R_, :].rearrange("p (h q) -> p h q", h=H),
                                                   in1=sm[R_, 2, :].unsqueeze(2).to_broadcast([TS, H, HP]), op=ALU.mult), r=[r_xdt, r_sm], w=[r_xw])
            for q in range(NSEQ_S):
                qi = q % 2
                rr = [r_hb[2 * qi], r_hb[2 * qi + 1]]
                fw.dma("act", h0f[qi], st_ssm[q].rearrange("(b m) n -> m b n", m=128), w=rr, stream="h0f%d" % qi)
                fw.op("dve", lambda h: h.tensor_scalar(out=Bm[qi][:], in0=B_tok[R_, :], scalar1=selv[:, q:q + 1], scalar2=None, op0=ALU.mult),
                      r=[r_B_tok, r_smp], w=[r_Bm[qi]])
                for blk in range(16):
                    bk = blk // 4
                    fw.op("pe", lambda h, blk=blk, bk=bk: h.matmul(pb[bk][:, (blk % 4) * 128:(blk % 4 + 1) * 128], lhsT=xw_tok[R_, blk * 128:(blk + 1) * 128],
                                                                  rhs=Bm[qi][:, bk * 128:(bk + 1) * 128], start=True, stop=True),
                          r=[r_xw, r_Bm[qi]], w=[pbr[bk]], sig=(blk % 4 == 3))
                fw.op("dve", lambda h: h.tensor_tensor(out=h0f[qi], in0=h0f[qi], in1=dec_fm[:, q, :].unsqueeze(2).to_broadcast([128, 16, 128]), op=ALU.mult),
                      r=rr + [r_dec], w=rr)
                for bk in range(4):
                    fw.op("dve", lambda h, bk=bk: h.tensor_tensor(out=h0f[qi][:, 4 * bk:4 * bk + 4, :], in0=h0f[qi][:, 4 * bk:4 * bk + 4, :],
                                                                  in1=pb[bk][:, :].rearrange("p (b n) -> p b n", b=4), op=ALU.add),
                          r=rr + [pbr[bk]], w=rr)
                fw.dma("sp", o_sssm[q].rearrange("(b m) n -> m b n", m=128), h0f[qi], r=rr, stream="h0f%d" % qi)
            fw.op("dve", lambda h: h.tensor_tensor(out=t1[R_, :], in0=t1[R_, :], in1=zs[R_, c, :], op=ALU.mult), r=[r_t1, r_zs[c]], w=[r_t1])
            norm_to_T(t1[R_, :], TS, r_t1, 2, xc, r_xc[0:NKD], Tm, (4, 5))

        bank_pairs = [(0, 1), (2, 3), (4, 5), (6, 7)]
        NBP = 3 if has_s else 4
        NBPcf = 2
        def pipelined(items, stage1, diag_fn, stage2):
            prev = None
            for it in items:
                if prev is not None:
                    diag_fn(prev)
                ctx = stage1(it)
                if prev is not None:
                    stage2(prev)
                prev = ctx
            if prev is not None:
                diag_fn(prev)
                stage2(prev)

        def xbc_s1(s):
            wv, rw, nk = w_next("fm")
            bp = bank_pairs[rot("bank", NBP)]
            fm_gemm(wv, rw, hT, r_hT[:len(slots)], bp)
            w_done()
            pi_ = rot("pre", 2)
            pr = pre[pi_]
            fw.op("dve", lambda h: h.tensor_copy(out=pr[:, 0:3], in_=halo_ssd[:, s, :]), r=[r_halo_ssd], w=[r_pre[pi_]])
            for ti, (c0, n) in enumerate(mtiles):
                fw.op("act", lambda h, ti=ti, c0=c0, n=n: h.copy(out=pr[:, 3 + c0:3 + c0 + n], in_=pb[bp[ti]][:, 0:n]), r=[pbr[bp[ti]]], w=[r_pre[pi_]])
            fw.op("dve", lambda h: h.tensor_copy(out=halo_ssd[:, s, :], in_=pr[:, Tm:Tm + 3]), r=[r_pre[pi_]], w=[r_halo_ssd])
            if last:
                fw.op("dve", lambda h: h.tensor_copy(out=st_pssdc[:, s, :], in_=pb[bp[LT]][:, LN - 3:LN]), r=[pbr[bp[LT]]], w=[r_st_pssdc])
            bp2 = bank_pairs[rot("bank", NBP)]
            so = None
            if has_s:
                so = sample_in(pr, r_pre[pi_], "ssd", s, bp[1], nxt=("ssd", s + 1) if s + 1 < NSX else None)
            return dict(s=s, pi=pi_, bp2=bp2, di=1 + rot("diag", 2), so=so)

        def xbc_diag(c):
            build_diag(diag[c["di"]], pssd_fm, c["s"], 4, r_pssd, r_diag[c["di"]])

        def xbc_s2(c):
            s_, bp2, di = c["s"], c["bp2"], c["di"]
            if c["so"] is not None:
                sample_out(c["so"])
            conv_pe(pre[c["pi"]], r_pre[c["pi"]], diag[di], r_diag[di], 4, 3, bp2)
            for ti, (c0, n) in enumerate(tiles):
                fw.op("act", lambda h, ti=ti, c0=c0, n=n: h.activation(out=xc[:, s_, c0:c0 + n], in_=pb[bp2[ti]][:, 0:n], func=AF.Silu,
                                                                     bias=pssd_fm[:, s_, 4:5], scale=1.0),
                      r=[pbr[bp2[ti]], r_pssd], w=[r_xc[s_]])

        pipelined(range(NSX if not prefix else 20), xbc_s1, xbc_diag, xbc_s2)
        if pi == 0:
            dbg_dump("xc", xc[:].rearrange("p k t -> p (k t)"), [128, NSX * TT], r_xc)

        def cf_s1(s):
            si = rot("sg", 2)
            pi_ = rot("pre", 2)
            pr = pre[pi_]
            wv, rw, nk = w_next("fm")
            bpg = bank_pairs[rot("bankcf", NBPcf)]
            fm_gemm(wv, rw, hT, r_hT[:len(slots)], bpg)
            w_done()
            for ti, (c0, n) in enumerate(tiles):
                fw.op("act", lambda h, ti=ti, c0=c0, n=n: h.activation(out=sg[si][:, c0:c0 + n], in_=pb[bpg[ti]][:, 0:n], func=AF.Sigmoid),
                      r=[pbr[bpg[ti]]], w=[r_sg[si]])
            wv, rw, nk = w_next("fm")
            bpa = bank_pairs[rot("bankcf", NBPcf)]
            fm_gemm(wv, rw, hT, r_hT[:len(slots)], bpa)
            w_done()
            fw.op("dve", lambda h: h.tensor_copy(out=pr[:, 0:30], in_=halo_cf[:, s, :]), r=[r_halo_cf], w=[r_pre[pi_]])
            for ti, (c0, n) in enumerate(mtiles):
                fw.op("dve", lambda h, ti=ti, c0=c0, n=n: h.tensor_tensor(out=pr[:, 30 + c0:30 + c0 + n], in0=pb[bpa[ti]][:, 0:n], in1=sg[si][:, c0:c0 + n], op=ALU.mult),
                      r=[pbr[bpa[ti]], r_sg[si]], w=[r_pre[pi_]])
            fw.op("dve", lambda h: h.tensor_copy(out=halo_cf[:, s, :], in_=pr[:, Tm:Tm + 30]), r=[r_pre[pi_]], w=[r_halo_cf])
            if last:
                fw.op("dve", lambda h: h.tensor_tensor(out=st_pcfc[:, s, :], in0=pb[bpa[LT]][:, LN - 30:LN], in1=sg[si][:, Tm - 30:Tm], op=ALU.mult),
                      r=[pbr[bpa[LT]], r_sg[si]], w=[r_st_pcfc])
            bp2 = bank_pairs[rot("bankcf", NBPcf)]
            so = None
            if has_s:
                so = sample_in(pr, r_pre[pi_], "cf", s, bpa[1], nxt=("cf", s + 1) if s + 1 < NKD else None, sgb=sg[si], r_sgb=r_sg[si])
            return dict(s=s, pi=pi_, bp2=bp2, so=so)

        def cf_diag(c):
            build_diag(diag[0], pcf_fm, c["s"], 31, r_pcf, r_diag[0], eng="act")

        def cf_s2(c):
            s_, bp2 = c["s"], c["bp2"]
            if c["so"] is not None:
                sample_out(c["so"])
            conv_pe(pre[c["pi"]], r_pre[c["pi"]], diag[0], r_diag[0], 31, 30, bp2)
            for ti, (c0, n) in enumerate(tiles):
                fw.op("act", lambda h, ti=ti, c0=c0, n=n: h.activation(out=mixcf[:, s_, c0:c0 + n], in_=pb[bp2[ti]][:, 0:n], func=AF.Identity,
                                                                     bias=pcf_fm[:, s_, 31:32], scale=1.0),
                      r=[pbr[bp2[ti]], r_pcf], w=[r_mixcf[s_]])

        def cf_layernorm():
            for ti, (c0, n) in enumerate(tiles):
                b1, b2 = (0, 1) if ti == 0 else (2, 3)
                for k in range(NKD):
                    sq = pre[k % 2]
                    fw.op("act", lambda h, k=k, sq=sq: h.activation(out=sq[:, 0:n], in_=mixcf[:, k, c0:c0 + n], func=AF.Square),
                          r=[r_mixcf[k]], w=[r_pre[k % 2]])
                    fw.op("pe", lambda h, k=k: h.matmul(pb[b1][:, 0:n], lhsT=ones_b[:], rhs=mixcf[:, k, c0:c0 + n], start=(k == 0), stop=(k == NKD - 1)),
                          r=[r_mixcf[k], r_ones], w=[pbr[b1]], sig=(k == NKD - 1))
                    fw.op("pe", lambda h, k=k, sq=sq: h.matmul(pb[b2][:, 0:n], lhsT=ones_b[:], rhs=sq[:, 0:n], start=(k == 0), stop=(k == NKD - 1)),
                          r=[r_pre[k % 2], r_ones], w=[pbr[b2]], sig=True)
                mean_bc = lnst[:, 0, c0:c0 + n]
                rstd_bc = lnst[:, 1, c0:c0 + n]
                fw.op("dve", lambda h: h.tensor_scalar(out=mean_bc, in0=pb[b1][:, 0:n], scalar1=1.0 / D, scalar2=None, op0=ALU.mult), r=[pbr[b1]], w=[r_lnst])
                fw.op("dve", lambda h: h.tensor_tensor(out=rstd_bc, in0=mean_bc, in1=mean_bc, op=ALU.mult), r=[r_lnst], w=[r_lnst])
                fw.op("dve", lambda h: h.scalar_tensor_tensor(out=rstd_bc, in0=pb[b2][:, 0:n], scalar=1.0 / D, in1=rstd_bc, op0=ALU.mult, op1=ALU.subtract),
                      r=[pbr[b2], r_lnst], w=[r_lnst])
                fw.op("act", lambda h: h.activation(out=rstd_bc, in_=rstd_bc, func=AF.Sqrt, bias=epsb[:, 0:1], scale=1.0), r=[r_lnst, r_eps], w=[r_lnst])
                fw.op("dve", lambda h: h.reciprocal(out=rstd_bc, in_=rstd_bc), r=[r_lnst], w=[r_lnst])
                for k in range(NKD):
                    tl = sg[k % 2]
                    fw.op("dve", lambda h, k=k, tl=tl: h.tensor_tensor(out=tl[:, 0:n], in0=mixcf[:, k, c0:c0 + n], in1=mean_bc, op=ALU.subtract),
                          r=[r_mixcf[k], r_lnst], w=[r_sg[k % 2]])
                    fw.op("dve", lambda h, k=k, tl=tl: h.tensor_tensor(out=tl[:, 0:n], in0=tl[:, 0:n], in1=rstd_bc, op=ALU.mult),
                          r=[r_sg[k % 2], r_lnst], w=[r_sg[k % 2]])
                    fw.op("act", lambda h, k=k, tl=tl: h.activation(out=mixcf[:, k, c0:c0 + n], in_=tl[:, 0:n], func=AF.Silu,
                                                                   scale=pcf_fm[:, k, 32:33], bias=pcf_fm[:, k, 33:34]),
                          r=[r_sg[k % 2], r_pcf], w=[r_mixcf[k]])
            if pi == 0:
                dbg_dump("mixcf", mixcf[:].rearrange("p k t -> p (k t)"), [128, NKD * TT], r_mixcf)

        cfgB = dict(xs=xs_tok, xdt=xdt_tok, t1=t1, r_xs=r_xs_tok, r_xdt=r_xdt, r_t1=r_t1,
                    R=[Rg[:, 0:512], Rg[:, 512:1024]], r_R=[r_Rg, r_Rg], ecs=ecs, r_ecs=r_ecs, nmt=2,
                    b_xs=(0, 1), b_B=2, b_misc=3, b_cs=(4, 5), b_y=(6, 7), b_nt=(0, 1))
        if with_sample:
            cfgM = dict(xs=hbuf[:, 0, :], xdt=hbuf[:, 1, :], t1=hbuf[:, 2:4, :].rearrange("p a f -> p (a f)").bitcast(F32),
                        r_xs=r_hb[0], r_xdt=r_hb[1], r_t1=r_hb[2],
                        R=[xsD[:, :], xsD[:, :]], r_R=[r_xsD, r_xsD], ecs=mtcs[:, 2048:4096].bitcast(F32), r_ecs=Res("ecsM"), nmt=1,
                        b_xs=(4, 5), b_B=7, b_misc=7, b_cs=(4, 5), b_y=(6, 6), b_nt=(4, 5))

        def ssd_block(c, rows, col0, Um, sample, cf=None, hooks=None):
            cf = cf or cfgB
            xs_tok, xdt_tok, t1 = cf["xs"], cf["xdt"], cf["t1"]
            xw_tok = xdt_tok
            r_xs_tok, r_xdt, r_t1 = cf["r_xs"], cf["r_xdt"], cf["r_t1"]
            r_xw = r_xdt
            ecs, r_ecs = cf["ecs"], cf["r_ecs"]
            bxs, bB, bm, bcs, bnt = cf["b_xs"], cf["b_B"], cf["b_misc"], cf["b_cs"], cf["b_nt"]
            cols = slice(col0, col0 + rows)
            R_ = slice(0, rows)
            for half in range(2):
                for j in range(8):
                    k = half * 8 + j
                    fw.op("pe", lambda h, j=j, k=k, half=half: h.transpose(out=pbf(bxs[half])[R_, j * 128:(j + 1) * 128], in_=xc[:, k, cols], identity=ident_b[:]),
                          r=[r_xc[k], r_ident], w=[pbr[bxs[half]]], sig=(j == 7))
                fw.op("act", lambda h, half=half: h.copy(out=xs_tok[R_, half * 1024:(half + 1) * 1024], in_=pbf(bxs[half])[R_, :]), r=[pbr[bxs[half]]], w=[r_xs_tok])
            for g in range(NG):
                fw.op("pe", lambda h, g=g: h.transpose(out=pbf(bB)[R_, g * 128:(g + 1) * 128], in_=xc[:, 16 + g, cols], identity=ident_b[:]),
                      r=[r_xc[16 + g], r_ident], w=[pbr[bB]], sig=(g == NG - 1))
            fw.op("dve", lambda h: h.tensor_copy(out=B_tok[R_, :], in_=pbf(bB)[R_, 0:512]), r=[pbr[bB]], w=[r_B_tok])
            dtc = dt_tok[R_, c, :]
            fw.op("dve", lambda h: h.tensor_tensor(out=sm[R_, 0, :], in0=dtc, in1=head_bc[R_, 1, :], op=ALU.mult), r=[r_dt[c], r_head], w=[r_sm])
            if not prefix:
                fw.op("dve", lambda h: h.tensor_tensor(out=xdt_tok[R_, :].rearrange("p (h q) -> p h q", h=H), in0=xs_tok[R_, :].rearrange("p (h q) -> p h q", h=H),
                                                       in1=dtc.unsqueeze(2).to_broadcast([rows, H, HP]), op=ALU.mult), r=[r_xs_tok, r_dt[c]], w=[r_xdt])
            fw.op("pe", lambda h: h.matmul(pb[bm][R_, 0:H], lhsT=Um[R_, R_], rhs=sm[R_, 0, :], start=True, stop=True), r=[r_U, r_sm], w=[pbr[bm]])
            if not prefix:
                fw.op("dve", lambda h: h.tensor_copy(out=dAh[R_, :], in_=sm[R_, 0, :]), r=[r_sm], w=[r_sm])
                fw.op("dve", lambda h: h.tensor_tensor(out=sm[R_, 4, :], in0=sm[R_, 0, :], in1=dAh[R_, :], op=ALU.subtract), r=[r_sm], w=[r_sm])
            fw.op("dve", lambda h: h.tensor_copy(out=sm[R_, 1, :], in_=pb[bm][R_, 0:H]), r=[pbr[bm]], w=[r_sm])
            if prefix:
                fw.op("pe", lambda h: h.matmul(pb[bm][:, 64:64 + H], lhsT=ones_f[:], rhs=sm[:, 0, :], start=True, stop=True), r=[r_ones, r_sm], w=[pbr[bm]])
                fw.op("dve", lambda h: h.tensor_copy(out=sm[:, 2, :], in_=pb[bm][:, 64:64 + H]), r=[pbr[bm]], w=[r_sm])
                fw.op("act", lambda h: h.activation(out=sm[:, 3, :], in_=pb[bm][:, 64:64 + H], func=AF.Exp), r=[pbr[bm]], w=[r_sm])
            v8 = lambda ap: ap.rearrange("p (a b) -> p a b", a=8)[:, :, R_]
            v4 = lambda ap: ap.rearrange("p (a b) -> p a b", a=4)[:, :, R_]

            def p1(g):
                for half in range(2):
                    Rh, r_Rh = cf["R"][half], cf["r_R"][half]
                    hh = slice(g * 8 + half * 4, g * 8 + half * 4 + 4)
                    Rb = Rh.bitcast(BF16)
                    for part, src in ((0, dAh[R_, hh]), (1, sm[R_, 4, hh])):
                        fw.op("dve", lambda h, part=part, src=src, Rb=Rb: h.tensor_tensor(out=Rb[R_, part * 512:(part + 1) * 512].rearrange("p (a b) -> p a b", a=4),
                                                                                     in0=src.unsqueeze(2).to_broadcast([rows, 4, 128]),
                                                                                     in1=Um[R_, :].unsqueeze(1).to_broadcast([rows, 4, 128]), op=ALU.mult),
                              r=[r_sm, r_U], w=[r_Rh])
                    for part in range(2):
                        fw.op("pe", lambda h, half=half, part=part, Rb=Rb: h.matmul(pb[bcs[half]][R_, :], lhsT=ones_b[R_, R_], rhs=Rb[R_, part * 512:(part + 1) * 512],
                                                                                 start=(part == 0), stop=(part == 1)),
                              r=[r_ones, r_Rh], w=[pbr[bcs[half]]], sig=(part == 1))
                fw.op("pe", lambda h, g=g: h.matmul(pb[bm][R_, 128:128 + rows], lhsT=xc[:, 16 + g, cols], rhs=xc[:, 20 + g, cols], start=True, stop=True),
                      r=[r_xc[16 + g], r_xc[20 + g]], w=[pbr[bm]])

            if not prefix:
                p1(0)
            for g in range(NG if not prefix else 0):
                gi = g % cf["nmt"]
                hs = slice(g * 8, (g + 1) * 8)
                if hooks:
                    hooks[0]()
                for half in range(2):
                    hh = slice(g * 8 + half * 4, g * 8 + half * 4 + 4)
                    bc = bcs[half]
                    fw.op("act", lambda h, half=half, bc=bc: h.activation(out=v4(ecs[R_, half * 512:(half + 1) * 512]), in_=v4(pb[bc][R_, :]), func=AF.Exp),
                          r=[pbr[bc]], w=[r_ecs])
                    for q4 in range(4):
                        hd = g * 8 + half * 4 + q4
                        o0 = half * 512 + q4 * 128
                        fw.op("dve", lambda h, bc=bc, q4=q4, hd=hd, o0=o0: h.tensor_scalar(out=seg[R_, o0:o0 + rows], in0=pb[bc][R_, q4 * 128:q4 * 128 + rows],
                                                                                        scalar1=sm[R_, 1, hd:hd + 1], scalar2=0.0, op0=ALU.subtract, op1=ALU.min),
                              r=[pbr[bc], r_sm], w=[r_seg], sig=(q4 == 3))
                    if not sample:
                        fw.op("dve", lambda h, bc=bc, hh=hh: h.tensor_copy(out=sm[R_, 2, hh], in_=pb[bc][R_, :].rearrange("p (a b) -> p a b", a=4)[:, :, rows - 1]),
                              r=[pbr[bc]], w=[r_sm])
                if not sample:
                    fw.op("dve", lambda h, hs=hs: h.tensor_copy(out=sm[R_, 3, hs], in_=ecs[R_, :].rearrange("p (a b) -> p a b", a=8)[:, :, rows - 1]), r=[r_ecs], w=[r_sm])
                fw.op("act", lambda h: h.activation(out=v8(seg[R_, :]), in_=v8(seg[R_, :]), func=AF.Exp), r=[r_seg], w=[r_seg])
                fw.op("dve", lambda h, gi=gi: h.tensor_tensor(out=CBTm[gi][R_, R_], in0=pb[bm][R_, 128:128 + rows], in1=Um[R_, R_], op=ALU.mult), r=[pbr[bm], r_U], w=[r_CBTm[gi]])
                fw.op("dve", lambda h, gi=gi: h.scalar_tensor_tensor(out=v8(MT[gi][R_, :]), in0=v8(seg[R_, :]),
                                                                     scalar=1.0, in1=CBTm[gi][R_, R_].unsqueeze(1).to_broadcast([rows, 8, rows]), op0=ALU.min, op1=ALU.mult),
                      r=[r_seg, r_CBTm[gi]], w=[r_MT[gi]])
                if not sample:
                    fw.op("dve", lambda h, gi=gi, g=g: h.tensor_tensor(out=Cs[gi][:].rearrange("p (a b) -> p a b", a=8), in0=ecs[:].rearrange("p (a b) -> p a b", a=8),
                                                                       in1=xc[:, 20 + g, cols].unsqueeze(1).to_broadcast([128, 8, 128]), op=ALU.mult),
                          r=[r_ecs, r_xc[20 + g]], w=[r_Cs[gi]])
                if hooks:
                    hooks[1]()
                if g + 1 < NG:
                    p1(g + 1)
                yb = cf["b_y"][g % 2]
                for hh in range(8):
                    hd = g * 8 + hh
                    fw.op("pe", lambda h, hh=hh, hd=hd, gi=gi, yb=yb: h.matmul(pb[yb][R_, hh * 64:(hh + 1) * 64], lhsT=MT[gi][R_, hh * 128:hh * 128 + rows],
                                                                           rhs=xdt_tok[R_, hd * 64:(hd + 1) * 64], start=True, stop=sample),
                          r=[r_MT[gi], r_xdt], w=[pbr[yb]], sig=(sample and hh == 7))
                    if not sample:
                        fw.op("pe", lambda h, hh=hh, g=g, gi=gi, yb=yb: h.matmul(pb[yb][:, hh * 64:(hh + 1) * 64], lhsT=Cs[gi][:, hh * 128:(hh + 1) * 128],
                                                                             rhs=hst_b[:, g, hh * 64:(hh + 1) * 64], start=False, stop=True),
                              r=[r_Cs[gi], r_hstb[g]], w=[pbr[yb]], sig=(hh == 7))
                fw.op("dve", lambda h, g=g, hs=hs: h.tensor_tensor(out=xsD[R_, :].rearrange("p (h q) -> p h q", h=8), in0=xs_tok[R_, g * 512:(g + 1) * 512].rearrange("p (h q) -> p h q", h=8),
                                                                   in1=head_bc[R_, 2, hs].unsqueeze(2).to_broadcast([rows, 8, HP]), op=ALU.mult), r=[r_xs_tok, r_head], w=[r_xsD])
                fw.op("dve", lambda h, g=g, yb=yb: h.tensor_tensor(out=t1[R_, g * 512:(g + 1) * 512], in0=pb[yb][R_, :], in1=xsD[R_, :], op=ALU.add),
                      r=[pbr[yb], r_xsD], w=[r_t1])
            if sample:
                return
            fw.op("dve", lambda h: h.tensor_tensor(out=sm[:, 2, :], in0=sm[:, 2, :], in1=sm[:, 1, :], op=ALU.subtract), r=[r_sm], w=[r_sm])
            fw.op("act", lambda h: h.activation(out=sm[:, 2, :], in_=sm[:, 2, :], func=AF.Exp), r=[r_sm], w=[r_sm])
            if prefix:
                fw.op("dve", lambda h: h.tensor_tensor(out=sm[:, 2, :], in0=sm[:, 2, :], in1=dtc, op=ALU.mult), r=[r_sm, r_dt[c]], w=[r_sm])
                fw.op("dve", lambda h: h.tensor_tensor(out=xw_tok[:].rearrange("p (h q) -> p h q", h=H), in0=xs_tok[:].rearrange("p (h q) -> p h q", h=H),
                                                       in1=sm[:, 2, :].unsqueeze(2).to_broadcast([128, H, HP]), op=ALU.mult), r=[r_xs_tok, r_sm], w=[r_xw])
            else:
                fw.op("dve", lambda h: h.tensor_tensor(out=xw_tok[:].rearrange("p (h q) -> p h q", h=H), in0=xdt_tok[:].rearrange("p (h q) -> p h q", h=H),
                                                       in1=sm[:, 2, :].unsqueeze(2).to_broadcast([128, H, HP]), op=ALU.mult), r=[r_xdt, r_sm], w=[r_xw])
            for g in range(NG):
                yb = bcs[g % 2]
                fw.op("pe", lambda h, g=g, yb=yb: h.matmul(pb[yb][:, :], lhsT=B_tok[:, g * 128:(g + 1) * 128], rhs=xw_tok[:, g * 512:(g + 1) * 512], start=True, stop=True),
                      r=[r_B_tok, r_xw], w=[pbr[yb]])
                fw.op("dve", lambda h, g=g: h.tensor_tensor(out=hst[:, g, :].rearrange("p (a b) -> p a b", a=8), in0=hst[:, g, :].rearrange("p (a b) -> p a b", a=8),
                                                            in1=sm[:, 3, g * 8:(g + 1) * 8].unsqueeze(2).to_broadcast([128, 8, HP]), op=ALU.mult),
                      r=[r_hst[g], r_sm], w=[r_hst[g]])
                fw.op("dve", lambda h, g=g, yb=yb: h.tensor_tensor(out=hst[:, g, :], in0=hst[:, g, :], in1=pb[yb][:, :], op=ALU.add), r=[r_hst[g], pbr[yb]], w=[r_hst[g]])
                if halo and c == 0:
                    fw.op("dve", lambda h, g=g: h.tensor_scalar(out=hst[:, g, :], in0=hst[:, g, :], scalar1=flg[:, 0:1], scalar2=None, op0=ALU.mult),
                          r=[r_hst[g], r_flg], w=[r_hst[g]])
                fw.op("act", lambda h, g=g: h.copy(out=hst_b[:, g, :], in_=hst[:, g, :]), r=[r_hst[g]], w=[r_hstb[g]])
            if prefix:
                return
            fw.op("dve", lambda h: h.tensor_tensor(out=t1[:], in0=t1[:], in1=zs[:, c, :], op=ALU.mult), r=[r_t1, r_zs[c]], w=[r_t1])
            norm_to_T(t1[:], 128, r_t1, 2, xc, r_xc[0:NKD], col0, bnt)

        if prefix:
            fw.barrier()
            for c in range(nch):
                ssd_block(c, 128, c * 128, U_f, False)
            continue
        cfst = dict(idx=0, prev=None, cur=None, did=False)

        def cf_hook_a():
            cfst["did"] = cfst["idx"] < NKD
            if cfst["did"]:
                if cfst["prev"] is not None:
                    cf_diag(cfst["prev"])
                cfst["cur"] = cf_s1(cfst["idx"])
                cfst["idx"] += 1

        def cf_hook_b():
            if cfst["did"]:
                if cfst["prev"] is not None:
                    cf_s2(cfst["prev"])
                cfst["prev"] = cfst["cur"]

        for c in range(nch):
            ssd_block(c, 128, c * 128, U_f, False, cf=cfgM, hooks=(cf_hook_a, cf_hook_b))
        while cfst["idx"] < NKD:
            cf_hook_a()
            cf_hook_b()
        if cfst["prev"] is not None:
            cf_diag(cfst["prev"])
            cf_s2(cfst["prev"])
        cf_layernorm()
        fw.barrier()
        if has_s:
            ssd_sample()
        if pi == 0:
            dbg_dump("mixssd", xc[:, 0:NKD, :].rearrange("p k t -> p (k t)"), [128, NKD * TT], r_xc)
        if last:
            dbg_dump("hst", hst[:].rearrange("p g f -> p (g f)"), [128, NG * 512], r_hst)

        fw.barrier()
        for (c, rows, col0) in slots:
            src_d = xp[chunks[c] * 128:(chunks[c] + 1) * 128, :] if c < nch else xsd
            fw.dma("sp", x1[0:rows, c, :], src_d, w=[r_x1[c]])
        for ct in range(4):
            for u in range(8):
                wv, rw, nk = w_next("tm")
                for kk in range(nk):
                    k = 4 * u + kk
                    for (c, rows, col0) in slots:
                        lhsT = xc[:, k, col0:col0 + rows] if k < NKD else mixcf[:, k - NKD, col0:col0 + rows]
                        rr = r_xc[k] if k < NKD else r_mixcf[k - NKD]
                        fw.op("pe", lambda h, c=c, k=k, kk=kk, lhsT=lhsT, rows=rows: h.matmul(pb[c][0:rows, :], lhsT=lhsT, rhs=wv[:, kk, :],
                                                                                           start=(k == 0), stop=(k == 2 * NKD - 1)),
                              r=[rr, rw], w=[pbr[c]], sig=(kk == nk - 1 and c == slots[-1][0]))
                w_done()
            for (c, rows, col0) in slots:
                fw.op("dve", lambda h, c=c, rows=rows: h.tensor_tensor(out=x1[0:rows, c, ct * 512:(ct + 1) * 512], in0=pb[c][0:rows, :],
                                                                      in1=x1[0:rows, c, ct * 512:(ct + 1) * 512], op=ALU.add),
                      r=[pbr[c], r_x1[c]], w=[r_x1[c]])
        if pi == 0:
            dbg_dump("x1", x1[:].rearrange("p c f -> p (c f)"), [128, NSLOT * D], r_x1)

        fw.barrier()
        for (c, rows, col0) in slots:
            norm_to_T(x1[0:rows, c, :], rows, r_x1[c], 1, h2T, [r_h2T[c]], col0, (6, 7), jnk=(junkD, r_junkD),
                      xnb=((xnD, r_xnD) if c % 2 else None))
        for (k0, k1) in ((0, KH), (KH, NKF)):
            def ffn_s1(item):
                s, which, si = item[0:3]
                strip = s + which * NKF
                wv, rw, nk = w_next("fm")
                bp = bank_pairs[rot("bank", NBP)]
                fm_gemm(wv, rw, h2T, r_h2T[:len(slots)], bp)
                w_done()
                pi_ = rot("pre", 2)
                pr = pre[pi_]
                fw.op("dve", lambda h: h.tensor_copy(out=pr[:, 0:2], in_=halo_ffn[:, strip, :]), r=[r_halo_ffn], w=[r_pre[pi_]])
                for ti, (c0, n) in enumerate(mtiles):
                    fw.op("act", lambda h, ti=ti, c0=c0, n=n: h.copy(out=pr[:, 2 + c0:2 + c0 + n], in_=pb[bp[ti]][:, 0:n]), r=[pbr[bp[ti]]], w=[r_pre[pi_]])
                if halo:
                    fw.op("dve", lambda h: h.tensor_scalar(out=pr[:, 2:2 + 128], in0=pr[:, 2:2 + 128], scalar1=flg[:, 0:1], scalar2=None, op0=ALU.mult),
                          r=[r_pre[pi_], r_flg], w=[r_pre[pi_]])
                fw.op("dve", lambda h: h.tensor_copy(out=halo_ffn[:, strip, :], in_=pr[:, Tm:Tm + 2]), r=[r_pre[pi_]], w=[r_halo_ffn])
                if last:
                    fw.op("dve", lambda h: h.tensor_copy(out=st_pffc[:, strip, :], in_=pb[bp[LT]][:, LN - 2:LN]), r=[pbr[bp[LT]]], w=[r_st_pffc])
                bp2 = bank_pairs[rot("bank", NBP)]
                so = None
                if has_s:
                    so = sample_in(pr, r_pre[pi_], "ffn", strip, bp[1], nxt=item[3] if len(item) > 3 else None)
                return dict(s=s, which=which, si=si, strip=strip, pi=pi_, bp2=bp2, di=1 + rot("diag", 2), so=so)

            def ffn_diag(c):
                build_diag(diag[c["di"]], pffn_fm, c["strip"], 3, r_pffn, r_diag[c["di"]])

            def ffn_s2(c):
                s_, which, si, strip, bp2, di = c["s"], c["which"], c["si"], c["strip"], c["bp2"], c["di"]
                if c["so"] is not None:
                    sample_out(c["so"])
                conv_pe(pre[c["pi"]], r_pre[c["pi"]], diag[di], r_diag[di], 3, 2, bp2)
                for ti, (c0, n) in enumerate(tiles):
                    if which == 0:
                        fw.op("act", lambda h, ti=ti, c0=c0, n=n: h.activation(out=sg[si][:, c0:c0 + n], in_=pb[bp2[ti]][:, 0:n], func=AF.Silu,
                                                                             bias=pffn_fm[:, strip, 3:4], scale=1.0),
                              r=[pbr[bp2[ti]], r_pffn], w=[r_sg[si]])
                    else:
                        fw.op("dve", lambda h, ti=ti, c0=c0, n=n: h.scalar_tensor_tensor(out=gv[:, s_ - k0, c0:c0 + n], in0=pb[bp2[ti]][:, 0:n],
                                                                                       scalar=pffn_fm[:, strip, 3:4], in1=sg[si][:, c0:c0 + n],
                                                                                       op0=ALU.add, op1=ALU.mult),
                              r=[pbr[bp2[ti]], r_pffn, r_sg[si]], w=[r_gv[s_ - k0]])

            items = []
            for s in range(k0, k1):
                si = rot("sg", 2)
                items += [(s, 0, si), (s, 1, si)]
            items = [it + ((("ffn", items[i + 1][0] + items[i + 1][1] * NKF),) if i + 1 < len(items) else ()) for i, it in enumerate(items)]
            pipelined(items, ffn_s1, ffn_diag, ffn_s2)
            if pi == 0 and k0 == 0:
                dbg_dump("gv", gv[:].rearrange("p k t -> p (k t)"), [128, KH * TT], r_gv)
            for ct in range(4):
                for ka in range(k0, k1, 4):
                    wv, rw, nk = w_next("tm")
                    for kk in range(nk):
                        k = ka + kk
                        for (c, rows, col0) in slots_out:
                            fw.op("pe", lambda h, c=c, k=k, kk=kk, rows=rows, col0=col0: h.matmul(pb[c][0:rows, :], lhsT=gv[:, k - k0, col0:col0 + rows], rhs=wv[:, kk, :],
                                                                                               start=(k == k0), stop=(k == k1 - 1)),
                                  r=[r_gv[k - k0], rw], w=[pbr[c]], sig=(kk == nk - 1 and c == slots_out[-1][0]))
                    w_done()
                for (c, rows, col0) in slots_out:
                    fw.op("dve", lambda h, c=c, rows=rows: h.tensor_tensor(out=x1[0:rows, c, ct * 512:(ct + 1) * 512], in0=pb[c][0:rows, :],
                                                                          in1=x1[0:rows, c, ct * 512:(ct + 1) * 512], op=ALU.add),
                          r=[pbr[c], r_x1[c]], w=[r_x1[c]])
        fw.dma("sp", wfin_bc, norm_fin.partition_broadcast(128), w=r_h2T + [r_wfin], stream="wfin")
        for (c, rows, col0) in slots_out:
            stat, r_stat = stat_set()
            fw.op("act", lambda h, c=c, rows=rows, stat=stat: h.activation(out=junk[0:rows, :], in_=x1[0:rows, c, :], func=AF.Square, accum_out=stat[0:rows, 4:5]),
                  r=[r_x1[c]], w=[r_junk, r_stat])
            fw.op("act", lambda h, rows=rows, stat=stat: h.activation(out=stat[0:rows, 5:6], in_=stat[0:rows, 4:5], func=AF.Sqrt, scale=1.0 / D, bias=epsb[0:rows, 0:1]),
                  r=[r_stat, r_eps], w=[r_stat])
            fw.op("dve", lambda h, rows=rows, stat=stat: h.reciprocal(out=stat[0:rows, 6:7], in_=stat[0:rows, 5:6]), r=[r_stat], w=[r_stat])
            fw.op("dve", lambda h, c=c, rows=rows, stat=stat: h.scalar_tensor_tensor(out=x1[0:rows, c, :], in0=x1[0:rows, c, :], scalar=stat[0:rows, 6:7], in1=wfin_bc[0:rows, :],
                                                                         op0=ALU.mult, op1=ALU.mult),
                  r=[r_x1[c], r_stat, r_wfin], w=[r_x1[c]])
            dst_d = y_p[yrows[c] * 128:(yrows[c] + 1) * 128, :] if c < nch else y_s
            fw.dma("sp", dst_d, x1[0:rows, c, :], r=[r_x1[c]])

    fw.barrier()

    stgE = [arena[:, i * 1024:(i + 1) * 1024].bitcast(F32) for i in range(8)]
    r_stgE = [Res("stgE%d" % i) for i in range(8)]
    ecnt = [0]

    def store_fm_rows(src_fm, R, nstrips, dst2d, r_src):
        for s0 in range(0, nstrips, 4):
            ns = min(4, nstrips - s0)
            e = ecnt[0] % 8
            ecnt[0] += 1
            for j in range(ns):
                fw.op("pe", lambda h, j=j: h.transpose(out=pb[e][0:R, j * 128:(j + 1) * 128], in_=src_fm[:, s0 + j, 0:R], identity=ident_f[:]),
                      r=[r_src, r_ident], w=[pbr[e]], sig=(j == ns - 1))
            fw.op("dve" if e % 2 == 0 else "act",
                  (lambda h: h.tensor_copy(out=stgE[e][0:R, 0:ns * 128], in_=pb[e][0:R, 0:ns * 128])) if e % 2 == 0 else
                  (lambda h: h.copy(out=stgE[e][0:R, 0:ns * 128], in_=pb[e][0:R, 0:ns * 128])),
                  r=[pbr[e]], w=[r_stgE[e]])
            fw.dma("sp", dst2d[:, s0 * 128:(s0 + ns) * 128], stgE[e][0:R, 0:ns * 128], r=[r_stgE[e]])

    store_fm_rows(st_pssdc, 3, NSX, o_pssdc, r_st_pssdc)
    store_fm_rows(st_pcfc, 30, NKD, o_pcfc, r_st_pcfc)
    store_fm_rows(st_pffc, 2, 2 * NKF, o_pffc, r_st_pffc)
    for g in range(NG):
        e = ecnt[0] % 8
        ecnt[0] += 1
        for j in range(4):
            fw.op("pe", lambda h, g=g, j=j: h.transpose(out=pb[e][:, j * 128:(j + 1) * 128], in_=hst[:, g, j * 128:(j + 1) * 128], identity=ident_f[:]),
                  r=[r_hst[g], r_ident], w=[pbr[e]], sig=(j == 3))
        fw.op("dve", lambda h: h.tensor_copy(out=stgE[e][:, :], in_=pb[e][:, :]), r=[pbr[e]], w=[r_stgE[e]])
        fw.dma("sp", o_pssm[g * 512:(g + 1) * 512, :].rearrange("(j q) n -> q j n", q=128), stgE[e][:, :].rearrange("q (j n) -> q j n", j=4), r=[r_stgE[e]])

    fw.wait_all("sp")
    nc.sync.drain() if False else None
    return nc, dbg_out


def make_consts():
    ident = np.eye(128, dtype=np.float32)
    U = np.triu(np.ones((128, 128), dtype=np.float32))
    idx = np.arange(128)
    same = ((idx[:, None] // 4) == (idx[None, :] // 4)) & (idx[:, None] < TS) & (idx[None, :] < TS)
    smp = np.zeros((128, 512), dtype=np.float32)
    smp[:, 0:128] = (same & (idx[:, None] <= idx[None, :])).astype(np.float32)
    smp[:, 128:256] = same.astype(np.float32)
    smp[:, 256:320] = 1.0
    smp[:, 448:512] = 1.0
    selbc = ((np.arange(TS)[None, :] // 4) == np.arange(NSEQ_S)[:, None]).astype(np.float32).reshape(-1)
    return ident, U, smp, selbc


def _core_inputs(inputs, core, consts):
    ident, U, smp, selbc = consts
    f = lambda a: np.ascontiguousarray(np.asarray(a, dtype=np.float32))
    s0 = core * NSEQ_S
    b, half = core // 2, core % 2
    xfull = f(inputs["x_prompt"][b])
    if half == 0:
        xp_c = np.concatenate([np.zeros((128, D), np.float32), xfull[0:1024]], axis=0)
        xpre_c = np.zeros((896, D), np.float32)
    else:
        xp_c = xfull[896:2048]
        xpre_c = xfull[0:896]
    m = {
        "xp": np.ascontiguousarray(xp_c), "xpre": np.ascontiguousarray(xpre_c),
        "c_flag": np.full((128,), float(half), np.float32),
        "xs": f(inputs["x_sample"][s0:s0 + NSEQ_S]).reshape(TS, D),
        "w_in": f(inputs["w_in"][0]), "w_out": f(inputs["w_out"][0]),
        "w_up": f(inputs["w_up"][0]), "w_down": f(inputs["w_down"][0]),
        "prm_norm": f(np.stack([inputs["norm_mix_w"][0], inputs["norm_ffn_w"][0], inputs["ssd_norm_w"][0]])),
        "prm_ssd": f(np.concatenate([inputs["ssd_conv_w"][0], inputs["ssd_conv_b"]], axis=0)),
        "prm_cf": f(np.concatenate([inputs["cf_conv_w"][0], inputs["cf_conv_b"], inputs["cf_ln_w"], inputs["cf_ln_b"]], axis=0)),
        "prm_ffn": f(np.concatenate([inputs["ffn_conv_w"][0], inputs["ffn_conv_b"]], axis=0)),
        "prm_head": f(np.concatenate([inputs["ssd_dt_bias"], inputs["ssd_a_log"], inputs["ssd_d"]], axis=0)),
        "norm_fin": f(inputs["norm_final_w"]),
        "c_ident": ident, "c_U": U, "c_smp": smp, "c_selbc": selbc,
        "st_ssm": f(inputs["state_ssm"][0, s0:s0 + NSEQ_S]).reshape(NSEQ_S, D, 128),
        "st_ssdc": f(inputs["state_ssd_conv"][0, s0:s0 + NSEQ_S]).reshape(NSEQ_S * 3, XBC),
        "st_cfc": f(inputs["state_cf_conv"][0, s0:s0 + NSEQ_S]).reshape(NSEQ_S * 30, D),
        "st_ffc": f(inputs["state_ffn_conv"][0, s0:s0 + NSEQ_S]).reshape(NSEQ_S * 2, 2 * FFN),
    }
    return m


_PASSES = [dict(mode="prefix", chunks=[0, 1, 2, 3]), dict(mode="prefix", chunks=[4, 5, 6]),
           dict(chunks=[0, 1, 2, 3, 4], halo=True, yrows=[None, 0, 1, 2, 3]),
           dict(chunks=[5, 6, 7, 8], yrows=[4, 5, 6, 7], sample=True, last=True)]


def kernel(**inputs):
    n = 8
    nc, _ = build_program(_PASSES, with_sample=True)
    consts = make_consts()
    in_maps = [_core_inputs(inputs, c, consts) for c in range(n)]
    res = run_bass_kernel_spmd(nc, in_maps, core_ids=list(range(n)))
    r = res.results
    f = lambda a: np.ascontiguousarray(np.asarray(a, dtype=np.float32))
    y_prompt = np.stack([np.concatenate([f(r[2 * b]["y_p"]), f(r[2 * b + 1]["y_p"])], axis=0) for b in range(4)])
    y_sample = np.concatenate([f(r[c]["y_s"]).reshape(NSEQ_S, 4, D) for c in range(n)], axis=0)
    p_ssm = np.stack([f(r[2 * b + 1]["o_pssm"]).reshape(H, HP, 128) for b in range(4)])[None]
    p_ssdc = np.stack([f(r[2 * b + 1]["o_pssdc"]) for b in range(4)])[None]
    p_cfc = np.stack([f(r[2 * b + 1]["o_pcfc"]) for b in range(4)])[None]
    p_ffc = np.stack([f(r[2 * b + 1]["o_pffc"]) for b in range(4)])[None]
    s_ssm = np.concatenate([f(r[c]["o_sssm"]).reshape(NSEQ_S, H, HP, 128) for c in range(n)], axis=0)[None]
    s_ssdc = np.concatenate([f(r[c]["o_sssdc"]).reshape(NSEQ_S, 3, XBC) for c in range(n)], axis=0)[None]
    s_cfc = np.concatenate([f(r[c]["o_scfc"]).reshape(NSEQ_S, 30, D) for c in range(n)], axis=0)[None]
    s_ffc = np.concatenate([f(r[c]["o_sffc"]).reshape(NSEQ_S, 2, 2 * FFN) for c in range(n)], axis=0)[None]
    return (y_prompt, y_sample, p_ssm, p_ssdc, p_cfc, p_ffc, s_ssm, s_ssdc, s_cfc, s_ffc)
```
